# Optimizing a Trainium2 kernel written in Bass

```python
import math
import jax, jax.numpy as jnp
from jax import lax
import numpy as np

D_MODEL = 1024
BATCH = 8
SEQ = 4096
DEPTH = 1

D_RNN = D_MODEL
RNN_BLOCKS = 16
RNN_BLOCK = D_RNN // RNN_BLOCKS
CONV_WIDTH = 4
LRU_C = 8.0
N_HEADS = 8
HEAD_DIM = D_MODEL // (2 * N_HEADS)
V_DIM = 2 * HEAD_DIM
D_QK = N_HEADS * 2 * HEAD_DIM
D_ATTN = N_HEADS * V_DIM
Q_BLOCK = 128
SPLIT_SIZES = (D_RNN, D_RNN, D_QK, D_QK, D_ATTN, D_ATTN, 2 * D_MODEL)
D_IN_TOTAL = sum(SPLIT_SIZES)
NORM_EPS = 1e-6

kernel_name = "hawk_diffattn_gated_parallel_block"


def lambda_init(layer_idx):
    return 0.8 - 0.6 * math.exp(-0.3 * layer_idx)


def rmsnorm(x, g):
    xf = x.astype(jnp.float32)
    y = xf * lax.rsqrt(jnp.mean(xf * xf, axis=-1, keepdims=True) + NORM_EPS)
    return (y * g.astype(jnp.float32)).astype(x.dtype)


def causal_depthwise_conv(x, w, b):
    S = x.shape[1]
    xp = jnp.pad(x, ((0, 0), (CONV_WIDTH - 1, 0), (0, 0)))
    y = xp[:, 0:S] * w[0]
    for tap in range(1, CONV_WIDTH):
        y = y + xp[:, tap:tap + S] * w[tap]
    return y + b


def rg_lru(x, wa, ba, wx, bx, a_param):
    B, S, _ = x.shape
    xf = x.astype(jnp.float32)
    xb = xf.reshape(B, S, RNN_BLOCKS, RNN_BLOCK)
    r = jax.nn.sigmoid(jnp.einsum('bsgi,gij->bsgj', xb, wa.astype(jnp.float32)) + ba).reshape(B, S, D_RNN)
    i = jax.nn.sigmoid(jnp.einsum('bsgi,gij->bsgj', xb, wx.astype(jnp.float32)) + bx).reshape(B, S, D_RNN)
    log_a = -LRU_C * r * jax.nn.softplus(-a_param.astype(jnp.float32))
    a = jnp.exp(log_a)
    b = jnp.sqrt(-jnp.expm1(2.0 * log_a)) * (i * xf)

    def combine(left, right):
        a1, b1 = left
        a2, b2 = right
        return a1 * a2, a2 * b1 + b2

    _, h = lax.associative_scan(combine, (a, b), axis=1)
    return h.astype(x.dtype)


def diff_attention(q, k, v, lam):
    S = q.shape[1]
    scale = HEAD_DIM ** -0.5
    outs = []
    for qb in range(S // Q_BLOCK):
        q0 = qb * Q_BLOCK
        q1 = q0 + Q_BLOCK
        s = jnp.einsum('bqhcd,bkhcd->bhcqk', q[:, q0:q1], k[:, :q1]).astype(jnp.float32) * scale
        mask = (q0 + jnp.arange(Q_BLOCK))[:, None] >= jnp.arange(q1)[None, :]
        s = jnp.where(mask, s, -jnp.inf)
        p = jax.nn.softmax(s, axis=-1)
        w = p[:, :, 0] - lam * p[:, :, 1]
        outs.append(jnp.einsum('bhqk,bkhe->bqhe', w.astype(v.dtype), v[:, :q1]))
    return jnp.concatenate(outs, axis=1)


def setup_inputs(seed: int = 0) -> dict:
    key = jax.random.key(seed)
    ks = jax.random.split(key, 24)
    f32 = jnp.float32
    nrm = lambda k, shape, s: jax.random.normal(k, shape, f32) * s
    x = jax.random.normal(ks[0], (BATCH, SEQ, D_MODEL), f32)
    pre_g = 1.0 + nrm(ks[1], (DEPTH, D_MODEL), 0.02)
    post_g = 1.0 + nrm(ks[2], (DEPTH, D_MODEL), 0.02)
    w_in = nrm(ks[3], (DEPTH, D_MODEL, D_IN_TOTAL), D_MODEL ** -0.5)
    conv_w = nrm(ks[4], (DEPTH, CONV_WIDTH, D_RNN), CONV_WIDTH ** -0.5)
    conv_b = nrm(ks[5], (DEPTH, D_RNN), 0.01)
    lru_wa = nrm(ks[6], (DEPTH, RNN_BLOCKS, RNN_BLOCK, RNN_BLOCK), RNN_BLOCK ** -0.5)
    lru_ba = nrm(ks[7], (DEPTH, RNN_BLOCKS, RNN_BLOCK), 0.01)
    lru_wx = nrm(ks[8], (DEPTH, RNN_BLOCKS, RNN_BLOCK, RNN_BLOCK), RNN_BLOCK ** -0.5)
    lru_bx = nrm(ks[9], (DEPTH, RNN_BLOCKS, RNN_BLOCK), 0.01)
    r0 = jax.random.uniform(ks[10], (DEPTH, D_RNN), f32, 0.9, 0.999)
    s0 = r0 ** (1.0 / LRU_C)
    lru_a = jnp.log(s0) - jnp.log1p(-s0)
    attn_lq1 = nrm(ks[11], (DEPTH, HEAD_DIM), 0.1)
    attn_lk1 = nrm(ks[12], (DEPTH, HEAD_DIM), 0.1)
    attn_lq2 = nrm(ks[13], (DEPTH, HEAD_DIM), 0.1)
    attn_lk2 = nrm(ks[14], (DEPTH, HEAD_DIM), 0.1)
    subln_g = 1.0 + nrm(ks[15], (DEPTH, V_DIM), 0.02)
    w_br_rnn = nrm(ks[16], (DEPTH, D_RNN, D_MODEL), D_RNN ** -0.5)
    w_br_attn = nrm(ks[17], (DEPTH, D_ATTN, D_MODEL), D_ATTN ** -0.5)
    w_out = nrm(ks[18], (DEPTH, D_MODEL, D_MODEL), D_MODEL ** -0.5)
    return {"x": x, "pre_g": pre_g, "post_g": post_g, "w_in": w_in,
            "conv_w": conv_w, "conv_b": conv_b, "lru_wa": lru_wa, "lru_ba": lru_ba,
            "lru_wx": lru_wx, "lru_bx": lru_bx, "lru_a": lru_a,
            "attn_lq1": attn_lq1, "attn_lk1": attn_lk1, "attn_lq2": attn_lq2,
            "attn_lk2": attn_lk2, "subln_g": subln_g, "w_br_rnn": w_br_rnn,
            "w_br_attn": w_br_attn, "w_out": w_out}


def reference(x, pre_g, post_g, w_in, conv_w, conv_b, lru_wa, lru_ba, lru_wx, lru_bx,
              lru_a, attn_lq1, attn_lk1, attn_lq2, attn_lk2, subln_g, w_br_rnn,
              w_br_attn, w_out):
    B, S, _ = x.shape
    split_points = [int(v) for v in np.cumsum(SPLIT_SIZES)[:-1]]
    h = x
    for l in range(DEPTH):
        u = rmsnorm(h, pre_g[l])
        z = jnp.einsum('bsd,df->bsf', u, w_in[l])
        xr, zr, q, k, v, za, gm = jnp.split(z, split_points, axis=-1)

        xc = causal_depthwise_conv(xr, conv_w[l], conv_b[l])
        y_r = rg_lru(xc, lru_wa[l], lru_ba[l], lru_wx[l], lru_bx[l], lru_a[l]) * jax.nn.silu(zr)

        lam_init = lambda_init(l)
        lam = (jnp.exp(jnp.sum(attn_lq1[l] * attn_lk1[l]).astype(jnp.float32))
               - jnp.exp(jnp.sum(attn_lq2[l] * attn_lk2[l]).astype(jnp.float32)) + lam_init)
        qh = q.reshape(B, S, N_HEADS, 2, HEAD_DIM)
        kh = k.reshape(B, S, N_HEADS, 2, HEAD_DIM)
        vh = v.reshape(B, S, N_HEADS, V_DIM)
        o = diff_attention(qh, kh, vh, lam)
        o = rmsnorm(o, subln_g[l]) * (1.0 - lam_init)
        y_a = o.reshape(B, S, D_ATTN) * jax.nn.silu(za)

        g_r, g_a = jnp.split(jax.nn.sigmoid(gm), 2, axis=-1)
        m = (g_r * jnp.einsum('bse,ed->bsd', y_r, w_br_rnn[l])
             + g_a * jnp.einsum('bse,ed->bsd', y_a, w_br_attn[l]))
        y = jnp.einsum('bsd,de->bse', m, w_out[l])
        h = h + rmsnorm(y, post_g[l])
    return h
```

```python
import math
from contextlib import ExitStack

import numpy as np
import ml_dtypes

import concourse.bass as bass
import concourse.mybir as mybir
from concourse.bass_utils import run_bass_kernel_spmd

F32 = mybir.dt.float32
BF16 = mybir.dt.bfloat16
AF = mybir.ActivationFunctionType
ALU = mybir.AluOpType
AX = mybir.AxisListType

P = 128
D = 1024
S = 4096
T = 512
NCH = S // T
NORM_EPS = 1e-6
LAM_INIT = 0.8 - 0.6 * math.exp(-0.3 * 0)
NW = 6
NA = 9
NE = 2


class Counter:
    def __init__(self, name, step):
        self.name, self.step, self.val, self.sem = name, step, 0, None


class Buf:
    __slots__ = ("name", "w", "r")

    def __init__(self, name=""):
        self.name, self.w, self.r = name, None, {}


class Sched:
    ENGS = ("pe", "act", "dve", "pool", "sp")

    def __init__(self):
        self.ops = {e: [] for e in self.ENGS}
        self.ctr = {e: Counter("c_" + e, 1) for e in self.ENGS}
        self.seen = {e: {} for e in self.ENGS}
        self.counters = list(self.ctr.values())

    def new_counter(self, name, step=16):
        c = Counter(name, step)
        self.counters.append(c)
        return c

    def _waits(self, eng, reads, writes):
        need = {}
        pe_c = self.ctr["pe"]
        seen = self.seen[eng]

        def add(c, v):
            if c is pe_c and eng == "pe":
                return
            if seen.get(c, 0) >= v:
                return
            if need.get(c, 0) < v:
                need[c] = v
        for b in reads:
            if b.w is not None:
                add(*b.w)
        for b in writes:
            if b.w is not None:
                add(*b.w)
            for c, v in b.r.items():
                add(c, v)
        for c, v in need.items():
            seen[c] = v
        return list(need.items())

    def op(self, eng, fn, reads=(), writes=(), flag=True):
        waits = self._waits(eng, reads, writes)
        c = self.ctr[eng]
        if flag:
            c.val += 1
            v = c.val
        else:
            v = c.val + 1
        for b in reads:
            if b.r.get(c, 0) < v:
                b.r[c] = v
        for b in writes:
            b.w = (c, v)
            b.r = {}
        self.ops[eng].append((fn, waits, c if flag else None))

    def dma(self, eng, fn, counter, reads=(), writes=()):
        waits = self._waits(eng, reads, writes)
        counter.val += counter.step
        v = counter.val
        for b in reads:
            if b.r.get(counter, 0) < v:
                b.r[counter] = v
        for b in writes:
            b.w = (counter, v)
            b.r = {}
        self.ops[eng].append((fn, waits, counter))

    def wait_all(self, eng, cvs):
        waits = []
        for c, v in cvs:
            if v > 0 and self.seen[eng].get(c, 0) < v:
                self.seen[eng][c] = v
                waits.append((c, v))
        self.ops[eng].append((None, waits, None))

    def emit(self, block):
        handles = {"pe": block.tensor, "act": block.scalar, "dve": block.vector,
                   "pool": block.gpsimd, "sp": block.sync}
        for e in self.ENGS:
            ops = self.ops[e]

            def body(eng, ops=ops):
                for fn, waits, c in ops:
                    for wc, wv in waits:
                        eng.wait_ge(wc.sem, wv)
                    if fn is None:
                        continue
                    ins = fn(eng)
                    if c is not None:
                        ins.then_inc(c.sem, c.step)
            handles[e](body)


def build_program(debug=False, nopump=False):
    nc = bass.Bass("TRN2", target_bir_lowering=False)

    def din(name, shape, dt=F32):
        return nc.dram_tensor(name, list(shape), dt, kind="ExternalInput").ap()

    x_d = din("x", [S, D])
    w_in_d = din("w_in_t", [64, P, 8, 128])
    w_brr_d = din("w_br_rnn_t", [8, P, 8, 128])
    w_bra_d = din("w_br_attn_t", [8, P, 8, 128])
    w_out_d = din("w_out_t", [8, P, 8, 128])
    preg_d = din("pre_g_b", [P, D])
    postg_d = din("post_g_b", [P, D])
    convw_d = din("conv_wt", [P, 8, 4])
    vecs_d = din("vecs", [P, 8, 4])
    wabd_d = din("wa_bd", [P, 8, 128])
    wxbd_d = din("wx_bd", [P, 8, 128])
    lamqk_d = din("lam_qk", [P, 4, 64])
    sublng_d = din("subln_g", [P, 1])
    ident_d = din("ident", [P, P], BF16)
    tri_d = din("tri", [P, P], BF16)
    onesb_d = din("ones_bf", [P, P], BF16)
    onesf_d = din("ones_f", [P, P])
    selb_d = din("selb", [P, 64], BF16)
    self_d = din("self", [64, 2 * P])
    out_d = nc.dram_tensor("out", [S, D], F32, kind="ExternalOutput").ap()

    sc = Sched()
    with ExitStack() as es:
        def sb(name, shape, dt):
            return es.enter_context(nc.sbuf_tensor(name, list(shape), dt))

        Kc = sb("Kc", [P, 8, S], BF16)
        Vc = sb("Vc", [P, S // P, D], BF16)
        uT = sb("uT", [P, 8, T], BF16)
        yr = sb("yr", [P, 8, T], BF16)
        ya = sb("ya", [P, 8, T], BF16)
        mT = sb("mT", [P, 8, T], BF16)
        wring = sb("wring", [P, NW, 8, 128], BF16)
        wabd = sb("wabd", [P, 8, 128], BF16)
        wxbd = sb("wxbd", [P, 8, 128], BF16)
        arena = sb("arena", [P, NA, 512], F32)
        Et = sb("Et", [P, NE, 2, T], BF16)
        qT = sb("qT", [P, 2, T], BF16)
        xrs = sb("xrs", [P, 1, 516], F32)
        convw = sb("convw", [P, 8, 4], F32)
        vecs = sb("vecs_s", [P, 8, 4], F32)
        lamqk = sb("lamqk", [P, 4, 64], F32)
        sublng = sb("sublng", [P, 1], F32)
        ident = sb("ident_s", [P, P], BF16)
        tri = sb("tri_s", [P, P], BF16)
        onesf = sb("onesf", [P, P], F32)
        selb = sb("selb_s", [P, 64], BF16)
        self_ = sb("self_s", [64, 2 * P], F32)
        small = sb("small", [P, 8, 8], F32)
        hstate = sb("hstate", [P, 8], F32)
        xcarry = sb("xcarry", [P, 8, 3], F32)
        misc = sb("misc", [P, 16], F32)
        stat = sb("stat", [P, 64], F32)
        ps = es.enter_context(nc.psum_tensor("ps", [P, 8, 512], F32))

        K_C, K_CH, K_HBA, K_HBX, K_T0, K_T1, K_T2, K_T3 = range(8)

        B = {}

        def buf(name):
            if name not in B:
                B[name] = Buf(name)
            return B[name]

        psB = [Buf("ps%d" % i) for i in range(8)]
        arB = [Buf("ar%d" % i) for i in range(NA)]
        arC = [sc.new_counter("arc%d" % i) for i in range(NA)]
        wrB = [Buf("wr%d" % i) for i in range(NW)]
        wrC = [sc.new_counter("wrc%d" % i) for i in range(NW)]
        EB = [Buf("E%d" % i) for i in range(NE)]
        qB = [Buf("q0"), Buf("q1")]
        xrB = [Buf("xr0")]
        uTB = [Buf("uT%d" % i) for i in range(4)]
        yrB = [Buf("yr%d" % i) for i in range(8)]
        yaB = [Buf("ya%d" % i) for i in range(8)]
        mTB = [Buf("mT%d" % i) for i in range(8)]
        KB = [[Buf("K%d_%d" % (h, j)) for j in range(NCH)] for h in range(8)]
        VB = [[Buf("V%d_%d" % (h, j)) for j in range(NCH)] for h in range(8)]
        statB = [Buf("st%d" % i) for i in range(64)]
        c_const = sc.new_counter("cconst")
        c_const2 = sc.new_counter("cconst2")
        const_bufs = []
        const_bufs2 = []

        state = {"ps": 0, "wr": 0, "st": 0, "e": 0, "pslim": 8}
        ar_free = list(range(NA))

        def ar_alloc():
            assert ar_free, "arena exhausted"
            i = ar_free.pop()
            return arena[:, i, :], [arB[i]], i

        def ar_alloc2():
            for i in range(0, NA - 1, 2):
                if i in ar_free and (i + 1) in ar_free:
                    ar_free.remove(i)
                    ar_free.remove(i + 1)
                    return arena[:, i:i + 2, :], [arB[i], arB[i + 1]], i
            raise AssertionError("arena pair exhausted")

        def ar_release(*idx):
            for i in idx:
                assert i not in ar_free
                ar_free.append(i)
            ar_free.sort()

        def ps_get():
            lim = state["pslim"]
            i = state["ps"] % lim
            state["ps"] = (i + 1) % lim
            return i

        def st_get():
            i = state["st"]
            state["st"] = (i + 1) % 64
            return stat[:, i:i + 1], [statB[i]]

        def wtile(src_ap):
            s = state["wr"]
            state["wr"] = (s + 1) % NW
            sc.dma("pool", lambda e, s=s, src_ap=src_ap: e.dma_start(out=wring[:, s, :, :], in_=src_ap),
                   wrC[s], writes=[wrB[s]])
            return s

        def cdma(dst, src, b):
            sc.dma("sp", lambda e: e.dma_start(out=dst, in_=src), c_const, writes=[b])
            const_bufs.append(b)

        cdma(convw[:], convw_d[:, :, :], buf("convw"))
        cdma(vecs[:], vecs_d[:, :, :], buf("vecs"))
        cdma(lamqk[:], lamqk_d[:, :, :], buf("lamqk"))
        cdma(sublng[:], sublng_d[:, :], buf("sublng"))
        cdma(ident[:], ident_d[:, :], buf("ident"))
        cdma(tri[:], tri_d[:, :], buf("tri"))
        cdma(onesf[:], onesf_d[:, :], buf("onesf"))
        cdma(selb[:], selb_d[:, :], buf("selb"))
        cdma(self_[:], self_d[:, :], buf("self"))
        for b in const_bufs:
            b.w = (c_const, c_const.val)

        def cdma2(dst, src, b):
            sc.dma("pool", lambda e: e.dma_start(out=dst, in_=src), c_const2, writes=[b])
            const_bufs2.append(b)

        cdma2(wabd[:], wabd_d[:, :, :], buf("wabd"))
        cdma2(wxbd[:], wxbd_d[:, :, :], buf("wxbd"))
        for b in const_bufs2:
            b.w = (c_const2, c_const2.val)

        bsmall = buf("small")
        bmisc = buf("misc")
        bh = buf("hstate")
        bxc = buf("xcarry")

        sc.op("dve", lambda e: e.memset(hstate[:], 0.0), writes=[bh])
        sc.op("dve", lambda e: e.memset(xcarry[:], 0.0), writes=[bxc])

        prod, prodB, prod_i = ar_alloc()
        sc.op("dve", lambda e: e.tensor_tensor(prod[:, 0:64], lamqk[:, 0, :], lamqk[:, 1, :], ALU.mult),
              reads=[buf("lamqk")], writes=prodB)
        sc.op("dve", lambda e: e.tensor_tensor(prod[:, 64:128], lamqk[:, 2, :], lamqk[:, 3, :], ALU.mult),
              reads=[buf("lamqk")] + prodB, writes=prodB)
        sc.op("dve", lambda e: e.reduce_sum(misc[:, 1:2], prod[:, 0:64], AX.X), reads=prodB, writes=[bmisc])
        sc.op("dve", lambda e: e.reduce_sum(misc[:, 2:3], prod[:, 64:128], AX.X), reads=prodB + [bmisc], writes=[bmisc])
        sc.op("act", lambda e: e.activation(misc[:, 3:5], misc[:, 1:3], AF.Exp), reads=[bmisc], writes=[bmisc])
        sc.op("dve", lambda e: e.tensor_tensor(misc[:, 5:6], misc[:, 4:5], misc[:, 3:4], ALU.subtract),
              reads=[bmisc], writes=[bmisc])
        sc.op("dve", lambda e: e.tensor_scalar(misc[:, 0:1], misc[:, 5:6], -LAM_INIT, None, ALU.add),
              reads=[bmisc], writes=[bmisc])
        sc.op("dve", lambda e: e.tensor_scalar(misc[:, 6:7], sublng[:, 0:1], 1.0 - LAM_INIT, None, ALU.mult),
              reads=[buf("sublng"), bmisc], writes=[bmisc])
        ar_release(prod_i)
        neg_lam = misc[:, 0:1]
        subg = misc[:, 6:7]

        Lap = vecs[:, :, 3]
        t0, t1, t2, t3 = (small[:, K_T0, :], small[:, K_T1, :], small[:, K_T2, :], small[:, K_T3, :])
        bv = buf("vecs")
        sc.op("act", lambda e: e.activation(t0, Lap, AF.Exp, scale=-1.0), reads=[bv], writes=[bsmall])
        sc.op("dve", lambda e: e.tensor_scalar(t1, t0, 1.0, None, ALU.add), reads=[bsmall], writes=[bsmall])
        sc.op("act", lambda e: e.activation(t2, t1, AF.Ln), reads=[bsmall], writes=[bsmall])
        sc.op("dve", lambda e: e.tensor_scalar(t3, t1, -1.0, None, ALU.add), reads=[bsmall], writes=[bsmall])
        sc.op("dve", lambda e: e.reciprocal(t3, t3), reads=[bsmall], writes=[bsmall])
        sc.op("dve", lambda e: e.tensor_tensor(t3, t3, t0, ALU.mult), reads=[bsmall], writes=[bsmall])
        sc.op("dve", lambda e: e.tensor_tensor(t3, t3, t2, ALU.mult), reads=[bsmall], writes=[bsmall])
        sc.op("dve", lambda e: e.tensor_scalar(small[:, K_C, :], t3, -8.0, None, ALU.mult), reads=[bsmall], writes=[bsmall])
        sc.op("dve", lambda e: e.tensor_scalar(small[:, K_CH, :], t3, -4.0, None, ALU.mult), reads=[bsmall], writes=[bsmall])
        sc.op("dve", lambda e: e.tensor_scalar(small[:, K_HBA, :], vecs[:, :, 1], 0.5, None, ALU.mult),
              reads=[bv, bsmall], writes=[bsmall])
        sc.op("dve", lambda e: e.tensor_scalar(small[:, K_HBX, :], vecs[:, :, 2], 0.5, None, ALU.mult),
              reads=[bv, bsmall], writes=[bsmall])

        def proj(bank, s, rhs_t, rhs_bufs, lo=0, hi=T):
            for dt in range(8):
                sc.op("pe", lambda e, dt=dt: e.matmul(ps[:, bank, lo:hi], wring[:, s, dt, :], rhs_t[:, dt, lo:hi],
                                                      start=(dt == 0), stop=(dt == 7)),
                      reads=[wrB[s]] + rhs_bufs, writes=[psB[bank]], flag=(dt == 7))

        XB = 7
        cw = buf("convw")

        xlock = [None]

        def xb_acquire(me):
            while xlock[0] is not None and xlock[0] != me:
                yield
            xlock[0] = me

        def xb_release():
            xlock[0] = None

        def rnn_gen(j):
            for c in range(8):
                xb_ = [xrB[0]]
                yield from xb_acquire("rnn")
                s_xr = wtile(w_in_d[c])
                sc.op("dve", lambda e, c=c: e.tensor_copy(xrs[:, 0, 0:3], xcarry[:, c, :]), reads=[bxc], writes=xb_)
                proj(XB, s_xr, uT, uTB)
                yield
                sc.op("dve", lambda e: e.tensor_copy(xrs[:, 0, 3:515], ps[:, XB, :]),
                      reads=[psB[XB]], writes=xb_)
                xb_release()
                yield
                yield from xb_acquire("rnn")
                s_zr = wtile(w_in_d[8 + c])
                proj(XB, s_zr, uT, uTB)
                sc.op("dve", lambda e, c=c: e.tensor_copy(xcarry[:, c, :], xrs[:, 0, 512:515]), reads=xb_, writes=[bxc])
                xc, xcB, xc_i = ar_alloc()
                sc.op("dve", lambda e, xc=xc, c=c: e.tensor_scalar(
                    xc, xrs[:, 0, 3:515], convw[:, c, 3:4], vecs[:, c, 0:1], ALU.mult, ALU.add),
                    reads=xb_ + [cw, bv], writes=xcB)
                yield
                tz, tzB, tz_i = ar_alloc()
                sc.op("act", lambda e, tz=tz: e.activation(tz, ps[:, XB, :], AF.Tanh, scale=0.5),
                      reads=[psB[XB]], writes=tzB)
                sc.op("dve", lambda e, xc=xc, c=c: e.scalar_tensor_tensor(
                    xc, xrs[:, 0, 2:514], convw[:, c, 2:3], xc, ALU.mult, ALU.add), reads=xb_ + [cw] + xcB, writes=xcB)
                yield
                sc.op("dve", lambda e, tz=tz, c=c: e.scalar_tensor_tensor(
                    yr[:, c, :], tz, 1.0, ps[:, XB, :], ALU.add, ALU.mult), reads=tzB + [psB[XB]], writes=[yrB[c]])
                xb_release()
                ar_release(tz_i)
                sc.op("dve", lambda e, xc=xc, c=c: e.scalar_tensor_tensor(
                    xc, xrs[:, 0, 1:513], convw[:, c, 1:2], xc, ALU.mult, ALU.add), reads=xb_ + [cw] + xcB, writes=xcB)
                yield
                sc.op("dve", lambda e, xc=xc, c=c: e.scalar_tensor_tensor(
                    xc, xrs[:, 0, 0:512], convw[:, c, 0:1], xc, ALU.mult, ALU.add), reads=xb_ + [cw] + xcB, writes=xcB)
                xcb_, xcbB, xcb_i = ar_alloc()
                xcb = xcb_.bitcast(BF16)[:, 0:T]
                sc.op("dve", lambda e, xcb=xcb, xc=xc: e.tensor_copy(xcb, xc), reads=xcB, writes=xcbB)
                yield
                yield from xb_acquire("rnn")
                sc.op("pe", lambda e, c=c, xcb=xcb: e.matmul(ps[:, XB, :], wabd[:, c, :], xcb, start=True, stop=True),
                      reads=[buf("wabd")] + xcbB, writes=[psB[XB]])
                yield
                tr, trB, tr_i = ar_alloc()
                sc.op("act", lambda e, tr=tr, c=c: e.activation(
                    tr, ps[:, XB, :], AF.Tanh, bias=small[:, K_HBA, c:c + 1], scale=0.5),
                    reads=[psB[XB], bsmall], writes=trB)
                xb_release()
                yield
                yield from xb_acquire("rnn")
                sc.op("pe", lambda e, c=c, xcb=xcb: e.matmul(ps[:, XB, :], wxbd[:, c, :], xcb, start=True, stop=True),
                      reads=[buf("wxbd")] + xcbB, writes=[psB[XB]])
                ar_release(xcb_i)
                a_, aB, a_i = ar_alloc()
                sc.op("act", lambda e, a_=a_, tr=tr, c=c: e.activation(
                    a_, tr, AF.Exp, bias=small[:, K_CH, c:c + 1], scale=small[:, K_CH, c:c + 1]),
                    reads=trB + [bsmall], writes=aB)
                yield
                ti, tiB, ti_i = ar_alloc()
                sc.op("act", lambda e, ti=ti, c=c: e.activation(
                    ti, ps[:, XB, :], AF.Tanh, bias=small[:, K_HBX, c:c + 1], scale=0.5),
                    reads=[psB[XB], bsmall], writes=tiB)
                xb_release()
                a2, a2B, a2_i = ar_alloc()
                sc.op("act", lambda e, a2=a2, tr=tr, c=c: e.activation(
                    a2, tr, AF.Exp, bias=small[:, K_C, c:c + 1], scale=small[:, K_C, c:c + 1]),
                    reads=trB + [bsmall], writes=a2B)
                yield
                th, thB, th_i = ar_alloc()
                sc.op("act", lambda e, th=th, tr=tr, c=c: e.activation(
                    th, tr, AF.Tanh, bias=small[:, K_CH, c:c + 1], scale=small[:, K_CH, c:c + 1]),
                    reads=trB + [bsmall], writes=thB)
                ar_release(tr_i)
                sc.op("dve", lambda e, ti=ti, xc=xc: e.scalar_tensor_tensor(ti, ti, 1.0, xc, ALU.add, ALU.mult),
                      reads=tiB + xcB, writes=tiB)
                ar_release(xc_i)
                yield
                sc.op("dve", lambda e, a2=a2, th=th: e.scalar_tensor_tensor(a2, a2, 1.0, th, ALU.add, ALU.mult),
                      reads=a2B + thB, writes=a2B)
                ar_release(th_i)
                yield
                sc.op("act", lambda e, a2=a2: e.activation(a2, a2, AF.Sqrt, scale=-0.25), reads=a2B, writes=a2B)
                yield
                sc.op("dve", lambda e, ti=ti, a2=a2: e.tensor_tensor(ti, ti, a2, ALU.mult), reads=tiB + a2B, writes=tiB)
                ar_release(a2_i)
                yield
                hh, hhB, hh_i = ar_alloc()
                sc.op("dve", lambda e, hh=hh, a_=a_, ti=ti, c=c: e.tensor_tensor_scan(
                    hh, a_, ti, hstate[:, c:c + 1], ALU.mult, ALU.add), reads=aB + tiB + [bh], writes=hhB)
                ar_release(a_i, ti_i)
                sc.op("dve", lambda e, hh=hh, c=c: e.tensor_copy(hstate[:, c:c + 1], hh[:, T - 1:T]),
                      reads=hhB, writes=[bh])
                yield
                sc.op("dve", lambda e, hh=hh, c=c: e.tensor_tensor(yr[:, c, :], hh, yr[:, c, :], ALU.mult),
                      reads=hhB + [yrB[c]], writes=[yrB[c]])
                ar_release(hh_i)
                yield

        def epi_gen(h, ssb, ssbB, ss_i, o0, o0B, o0_i, o1, o1B, o1_i):
            sc.op("dve", lambda e: e.reciprocal(ssb[0:64, :], ssb[0:64, :]), reads=ssbB, writes=ssbB)
            yield
            yield from xb_acquire("epi")
            sc.op("pe", lambda e: e.matmul(ps[:, XB, :], self_[0:64, 0:P], ssb[0:64, :], start=True, stop=True),
                  reads=ssbB + [buf("self")], writes=[psB[XB]])
            yield
            sc.op("dve", lambda e: e.tensor_tensor(o0, o0, ps[:, XB, :], ALU.mult), reads=o0B + [psB[XB]], writes=o0B)
            xb_release()
            yield
            yield from xb_acquire("epi")
            sc.op("pe", lambda e: e.matmul(ps[:, XB, :], self_[0:64, P:2 * P], ssb[0:64, :], start=True, stop=True),
                  reads=ssbB + [buf("self")], writes=[psB[XB]])
            yield
            sc.op("dve", lambda e: e.tensor_tensor(o1, o1, ps[:, XB, :], ALU.mult), reads=o1B + [psB[XB]], writes=o1B)
            xb_release()
            sc.op("dve", lambda e: e.scalar_tensor_tensor(o0, o1, neg_lam, o0, ALU.mult, ALU.add),
                  reads=o0B + o1B + [bmisc], writes=o0B)
            yield
            sc.op("dve", lambda e: e.tensor_tensor(o1, o0, o0, ALU.mult), reads=o0B, writes=o1B)
            yield
            yield from xb_acquire("epi")
            sc.op("pe", lambda e: e.matmul(ps[:, XB, :], onesf[:], o1, start=True, stop=True),
                  reads=o1B + [buf("onesf")], writes=[psB[XB]])
            yield
            sc.op("act", lambda e: e.activation(ssb, ps[:, XB, :], AF.Ln, bias=NORM_EPS), reads=[psB[XB]], writes=ssbB)
            xb_release()
            ar_release(o1_i)
            sc.op("act", lambda e: e.activation(ssb, ssb, AF.Exp, scale=-0.5), reads=ssbB, writes=ssbB)
            yield
            sc.op("dve", lambda e: e.tensor_tensor(o0, o0, ssb, ALU.mult), reads=o0B + ssbB, writes=o0B)
            ar_release(ss_i)
            yield
            sc.op("dve", lambda e: e.scalar_tensor_tensor(ya[:, h, :], o0, subg, ya[:, h, :], ALU.mult, ALU.mult),
                  reads=o0B + [bmisc, yaB[h]], writes=[yaB[h]])
            ar_release(o0_i)
            yield

        bg = {"qk": None, "epi": None, "rnn": None}

        def pump(n=1):
            if nopump:
                return
            for _ in range(n):
                for k in ("qk", "epi", "rnn"):
                    g = bg[k]
                    if g is not None:
                        try:
                            next(g)
                        except StopIteration:
                            bg[k] = None

        def drain(k):
            while bg[k] is not None:
                for kk in ("qk", "epi", "rnn"):
                    g = bg[kk]
                    if g is not None:
                        try:
                            next(g)
                        except StopIteration:
                            bg[kk] = None

        for j in range(NCH):
            tok0 = j * T
            state["pslim"] = 8
            pg, pgB, pgi = ar_alloc2()
            pgf = pg.rearrange("p a b -> p (a b)")
            sc.dma("sp", lambda e, pgf=pgf: e.dma_start(out=pgf, in_=preg_d[:, :]), arC[pgi], writes=pgB)
            s1 = {}

            def s1_front(tt):
                r0 = tok0 + tt * P
                xt, xtB, xi = ar_alloc2()
                xf = xt.rearrange("p a b -> p (a b)")
                sc.dma("sp", lambda e: e.dma_start(out=xf, in_=x_d[r0:r0 + P, :]), arC[xi], writes=xtB)
                ub, ubB, ub_i = ar_alloc()
                ubb = ub.bitcast(BF16)
                ss, ssB = st_get()
                sc.op("act", lambda e: e.activation(ubb, xf, AF.Square, accum_out=ss), reads=xtB, writes=ubB + ssB)
                sd, sdB = st_get()
                sc.op("act", lambda e: e.activation(sd, ss, AF.Sqrt, bias=NORM_EPS, scale=1.0 / D),
                      reads=ssB, writes=sdB)
                rs, rsB = st_get()
                sc.op("dve", lambda e: e.reciprocal(rs, sd), reads=sdB, writes=rsB)
                sc.op("dve", lambda e, pgf=pgf: e.scalar_tensor_tensor(ubb, xf, rs, pgf, ALU.mult, ALU.mult),
                      reads=xtB + rsB + pgB + ubB, writes=ubB)
                s1[tt] = (xi, ubb, ubB, ub_i)

            def s1_back(tt):
                xi, ubb, ubB, ub_i = s1.pop(tt)
                ar_release(xi, xi + 1)
                bank = ps_get()
                pst = ps[:, bank, :].bitcast(BF16)
                for dt in range(8):
                    sc.op("pe", lambda e, dt=dt: e.transpose(
                        pst[:, dt * P:(dt + 1) * P], ubb[:, dt * P:(dt + 1) * P], ident[:]),
                        reads=ubB + [buf("ident")], writes=[psB[bank]], flag=(dt == 7))
                ar_release(ub_i)
                sc.op("dve", lambda e: e.tensor_copy(
                    uT[:, :, tt * P:(tt + 1) * P], pst.rearrange("p (a b) -> p a b", a=8)),
                    reads=[psB[bank]], writes=[uTB[tt]])

            s1_front(0)
            s1_front(1)
            s1_back(0)
            s1_front(2)
            s1_back(1)
            s1_front(3)
            s1_back(2)
            s1_back(3)
            ar_release(pgi, pgi + 1)

            bg["rnn"] = rnn_gen(j)
            n_steps = 16 + 8 * (4 * j + 4)
            npump = max(1, -(-(8 * 17) // n_steps))
            state["pslim"] = 6
            state["ps"] = 0

            for h in range(8):
                s_za = wtile(w_in_d[40 + h])
                b_za = ps_get()
                proj(b_za, s_za, uT, uTB)
                tza, tzaB, tza_i = ar_alloc()
                sc.op("act", lambda e, tza=tza, b_za=b_za: e.activation(tza, ps[:, b_za, :], AF.Tanh, scale=0.5),
                      reads=[psB[b_za]], writes=tzaB)
                sc.op("dve", lambda e, tza=tza, b_za=b_za, h=h: e.scalar_tensor_tensor(
                    ya[:, h, :], tza, 1.0, ps[:, b_za, :], ALU.add, ALU.mult),
                    reads=tzaB + [psB[b_za]], writes=[yaB[h]])
                ar_release(tza_i)
                pump(npump)
            for h in range(8):
                s_v = wtile(w_in_d[32 + h])
                b_v = ps_get()
                for tt in range(4):
                    for dt in range(8):
                        sc.op("pe", lambda e, b_v=b_v, tt=tt, dt=dt, s_v=s_v: e.matmul(
                            ps[:, b_v, tt * P:(tt + 1) * P], uT[:, dt, tt * P:(tt + 1) * P], wring[:, s_v, dt, :],
                            start=(dt == 0), stop=(dt == 7)),
                            reads=[wrB[s_v], uTB[tt]], writes=[psB[b_v]], flag=(dt == 7))
                sc.op("dve", lambda e, b_v=b_v, h=h, j=j: e.tensor_copy(
                    Vc[:, 4 * j:4 * j + 4, h * P:(h + 1) * P], ps[:, b_v, :].rearrange("p (a b) -> p a b", a=4)),
                    reads=[psB[b_v]], writes=[VB[h][j]])
                pump(npump)
            nkt = 4 * j + 4

            def qk_inline(h):
                s_q = wtile(w_in_d[16 + h])
                s_k = wtile(w_in_d[24 + h])
                proj(0, s_q, uT, uTB)
                proj(2, s_k, uT, uTB)
                qq = h % 2
                sc.op("dve", lambda e: e.tensor_copy(qT[:, qq, :], ps[:, 0, :]), reads=[psB[0]], writes=[qB[qq]])
                sc.op("dve", lambda e, tok0=tok0: e.tensor_copy(Kc[:, h, tok0:tok0 + T], ps[:, 2, :]),
                      reads=[psB[2]], writes=[KB[h][j]])

            def qk_gen(h):
                qq = h % 2
                yield from xb_acquire("qk")
                s_q = wtile(w_in_d[16 + h])
                proj(XB, s_q, uT, uTB)
                yield
                sc.op("dve", lambda e: e.tensor_copy(qT[:, qq, :], ps[:, XB, :]), reads=[psB[XB]], writes=[qB[qq]])
                xb_release()
                yield
                yield from xb_acquire("qk")
                s_k = wtile(w_in_d[24 + h])
                proj(XB, s_k, uT, uTB)
                yield
                sc.op("dve", lambda e, tok0=tok0: e.tensor_copy(Kc[:, h, tok0:tok0 + T], ps[:, XB, :]),
                      reads=[psB[XB]], writes=[KB[h][j]])
                xb_release()
                yield

            def qk(h, kt, gi):
                i = kt - 4 * j
                lo = P * i if i > 0 else 0
                pr = gi % 2
                qq = h % 2
                for c2 in range(2):
                    bank = 2 * pr + c2
                    sc.op("pe", lambda e, bank=bank, c2=c2: e.matmul(
                        ps[:, bank, lo:T], Kc[64 * c2:64 * c2 + 64, h, kt * P:(kt + 1) * P],
                        qT[64 * c2:64 * c2 + 64, qq, lo:T], start=True, stop=True),
                        reads=[KB[h][kt // 4], qB[qq]], writes=[psB[bank]])

            steps = [(h, kt) for h in range(8) for kt in range(nkt)]
            qk_inline(0)
            qk(0, 0, 0)
            for gi, (h, kt) in enumerate(steps):
                if kt == 0 and h + 1 < 8:
                    bg["qk"] = qk_gen(h + 1)
                if gi + 1 < len(steps):
                    h2, kt2 = steps[gi + 1]
                    if kt2 == 0:
                        drain("qk")
                    qk(h2, kt2, gi + 1)
                i = kt - 4 * j
                lo = P * i if i > 0 else 0
                pr = gi % 2
                ei = state["e"]
                state["e"] = (ei + 1) % NE
                sc.op("act", lambda e, ei=ei, pr=pr, lo=lo: e.activation(
                    Et[:, ei, :, lo:T], ps[:, 2 * pr:2 * pr + 2, lo:T], AF.Exp, scale=0.125),
                    reads=[psB[2 * pr], psB[2 * pr + 1]], writes=[EB[ei]])
                if i >= 0:
                    for c2 in range(2):
                        sc.op("dve", lambda e, ei=ei, c2=c2, lo=lo: e.tensor_tensor(
                            Et[:, ei, c2, lo:lo + P], Et[:, ei, c2, lo:lo + P], tri[:], ALU.mult),
                            reads=[EB[ei], buf("tri")], writes=[EB[ei]])
                first, last = (kt == 0), (kt == nkt - 1)
                for c2 in range(2):
                    sc.op("pe", lambda e, ei=ei, c2=c2, kt=kt, lo=lo, first=first, last=last, h=h: e.matmul(
                        ps[:, 4 + c2, lo:T], Vc[:, kt, h * P:(h + 1) * P], Et[:, ei, c2, lo:T],
                        start=first, stop=last),
                        reads=[EB[ei], VB[h][kt // 4]], writes=[psB[4 + c2]], flag=False)
                for c2 in range(2):
                    sc.op("pe", lambda e, ei=ei, c2=c2, lo=lo, first=first, last=last: e.matmul(
                        ps[32 * c2:32 * c2 + 32, 6, lo:T], selb[:, 32 * c2:32 * c2 + 32], Et[:, ei, c2, lo:T],
                        start=first, stop=last),
                        reads=[EB[ei], buf("selb")], writes=[psB[6]], flag=(c2 == 1))
                pump(npump)
                if last:
                    drain("epi")
                    ssb, ssbB, ss_i = ar_alloc()
                    o0, o0B, o0_i = ar_alloc()
                    o1, o1B, o1_i = ar_alloc()
                    sc.op("dve", lambda e, ssb=ssb: e.tensor_copy(ssb[0:64, :], ps[0:64, 6, :]),
                          reads=[psB[6]], writes=ssbB)
                    sc.op("act", lambda e, o0=o0: e.activation(o0, ps[:, 4, :], AF.Copy), reads=[psB[4]], writes=o0B)
                    sc.op("dve", lambda e, o1=o1: e.tensor_copy(o1, ps[:, 5, :]), reads=[psB[5]], writes=o1B)
                    bg["epi"] = epi_gen(h, ssb, ssbB, ss_i, o0, o0B, o0_i, o1, o1B, o1_i)
            drain("rnn")
            state["pslim"] = 7
            state["ps"] = 0

            s5x = {}

            def s5_load(tt):
                r0 = tok0 + tt * P
                xt, xtB, xi = ar_alloc2()
                xf = xt.rearrange("p a b -> p (a b)")
                sc.dma("sp", lambda e: e.dma_start(out=xf, in_=x_d[r0:r0 + P, :]), arC[xi], writes=xtB)
                s5x[tt] = (xt, xf, xtB, xi)

            for dt in range(8):
                s_gr = wtile(w_in_d[48 + dt])
                s_ga = wtile(w_in_d[56 + dt])
                s_br = wtile(w_brr_d[dt])
                s_ba = wtile(w_bra_d[dt])
                b_gr = ps_get()
                proj(b_gr, s_gr, uT, uTB)
                pump(2)
                b_ga = ps_get()
                proj(b_ga, s_ga, uT, uTB)
                pump(2)
                b_pr = ps_get()
                proj(b_pr, s_br, yr, yrB)
                if dt == 0:
                    drain("epi")
                    pg, pgB, pgi = ar_alloc2()
                    sc.dma("sp", lambda e, pg=pg: e.dma_start(out=pg.rearrange("p a b -> p (a b)"), in_=postg_d[:, :]),
                           arC[pgi], writes=pgB)
                    s5_load(0)
                    s5_load(1)
                b_pa = ps_get()
                proj(b_pa, s_ba, ya, yaB)
                tg, tgB, tg_i = ar_alloc()
                tg2, tg2B, tg2_i = ar_alloc()
                sc.op("act", lambda e, tg=tg, b_gr=b_gr: e.activation(tg, ps[:, b_gr, :], AF.Tanh, scale=0.5),
                      reads=[psB[b_gr]], writes=tgB)
                sc.op("act", lambda e, tg2=tg2, b_ga=b_ga: e.activation(tg2, ps[:, b_ga, :], AF.Tanh, scale=0.5),
                      reads=[psB[b_ga]], writes=tg2B)
                sc.op("dve", lambda e, tg=tg, b_pr=b_pr: e.scalar_tensor_tensor(
                    tg, tg, 1.0, ps[:, b_pr, :], ALU.add, ALU.mult), reads=tgB + [psB[b_pr]], writes=tgB)
                sc.op("dve", lambda e, tg2=tg2, b_pa=b_pa: e.scalar_tensor_tensor(
                    tg2, tg2, 1.0, ps[:, b_pa, :], ALU.add, ALU.mult), reads=tg2B + [psB[b_pa]], writes=tg2B)
                sc.op("dve", lambda e, tg=tg, tg2=tg2, dt=dt: e.tensor_tensor(mT[:, dt, :], tg, tg2, ALU.add),
                      reads=tgB + tg2B, writes=[mTB[dt]])
                ar_release(tg_i, tg2_i)
            state["pslim"] = 8

            for ft in range(8):
                s_o = wtile(w_out_d[ft])
                for tt in range(4):
                    bank = 2 * tt + ft // 4
                    c0 = (ft % 4) * P
                    for dt in range(8):
                        sc.op("pe", lambda e, bank=bank, c0=c0, dt=dt, tt=tt, s_o=s_o: e.matmul(
                            ps[:, bank, c0:c0 + P], mT[:, dt, tt * P:(tt + 1) * P], wring[:, s_o, dt, :],
                            start=(dt == 0), stop=(dt == 7)),
                            reads=[mTB[dt], wrB[s_o]], writes=[psB[bank]], flag=(dt == 7))
            junk, junkB, junk_i = ar_alloc()
            junkb = junk.bitcast(BF16).rearrange("p (a b) -> p a b", a=2)
            s5r = {}

            def s5_front(tt):
                yps = ps[:, 2 * tt:2 * tt + 2, :]
                ypsB = [psB[2 * tt], psB[2 * tt + 1]]
                ss, ssB = st_get()
                sc.op("act", lambda e, junkb=junkb: e.activation(junkb, yps, AF.Square, accum_out=ss), reads=ypsB, writes=junkB + ssB)
                sd, sdB = st_get()
                sc.op("act", lambda e: e.activation(sd, ss, AF.Sqrt, bias=16.0 * NORM_EPS, scale=1.0 / D),
                      reads=ssB, writes=sdB)
                rs, rsB = st_get()
                sc.op("dve", lambda e: e.reciprocal(rs, sd), reads=sdB, writes=rsB)
                s5r[tt] = (rs, rsB)

            def s5_back(tt):
                r0 = tok0 + tt * P
                yps = ps[:, 2 * tt:2 * tt + 2, :]
                ypsB = [psB[2 * tt], psB[2 * tt + 1]]
                rs, rsB = s5r.pop(tt)
                xt, xf, xtB, xi = s5x.pop(tt)
                sc.op("dve", lambda e, pg=pg: e.tensor_tensor(yps, yps, pg, ALU.mult), reads=ypsB + pgB, writes=ypsB)
                sc.op("dve", lambda e: e.scalar_tensor_tensor(xt, yps, rs, xt, ALU.mult, ALU.add),
                      reads=ypsB + rsB + xtB, writes=xtB)
                sc.dma("sp", lambda e: e.dma_start(out=out_d[r0:r0 + P, :], in_=xf), arC[xi], reads=xtB)
                ar_release(xi, xi + 1)

            s5_front(0)
            s5_load(2)
            s5_front(1)
            s5_back(0)
            s5_load(3)
            s5_front(2)
            s5_back(1)
            s5_front(3)
            s5_back(2)
            s5_back(3)
            ar_release(junk_i)
            ar_release(pgi, pgi + 1)

        if debug:
            c_dbg = sc.new_counter("cdbg")
            def dump(name, t_ap, shape, dt, bufs):
                d = nc.dram_tensor(name, list(shape), dt, kind="ExternalOutput").ap()
                sc.dma("sp", lambda e: e.dma_start(out=d, in_=t_ap), c_dbg, reads=bufs)
            dump("dbg_uT", uT[:], [P, 8, T], BF16, uTB)
            dump("dbg_yr", yr[:], [P, 8, T], BF16, yrB)
            dump("dbg_ya", ya[:], [P, 8, T], BF16, yaB)
            dump("dbg_mT", mT[:], [P, 8, T], BF16, mTB)
            dump("dbg_Kc", Kc[:], [P, 8, S], BF16, [b for l in KB for b in l])
            dump("dbg_Vc", Vc[:], [P, S // P, D], BF16, [b for l in VB for b in l])
            dump("dbg_small", small[:], [P, 8, 8], F32, [bsmall])
            dump("dbg_misc", misc[:], [P, 16], F32, [bmisc])
            dump("dbg_hstate", hstate[:], [P, 8], F32, [bh])
            dump("dbg_stat", stat[:], [P, 64], F32, statB)
            sc.wait_all("sp", [(c_dbg, c_dbg.val)])
        sc.wait_all("sp", [(c, c.val) for c in arC])

        for c in sc.counters:
            c.sem = es.enter_context(nc.semaphore(c.name))
        block = es.enter_context(nc.Block())
        sc.emit(block)
    return nc


def _tile_w(w, nft):
    return np.ascontiguousarray(w.reshape(8, P, nft, 128).transpose(2, 1, 0, 3))


def _selb():
    return np.ones((P, 64), np.float32).astype(ml_dtypes.bfloat16)


def _self():
    m = np.zeros((64, 2 * P), np.float32)
    m[0, 0:P] = 1.0
    m[32, P:2 * P] = 1.0
    return m


def _blockdiag(w):
    out = np.zeros((P, 8, P), np.float32)
    for g in range(16):
        c, o = g // 2, (g % 2) * 64
        out[o:o + 64, c, o:o + 64] = w[g]
    return out


def kernel(x, pre_g, post_g, w_in, conv_w, conv_b, lru_wa, lru_ba, lru_wx, lru_bx,
           lru_a, attn_lq1, attn_lk1, attn_lq2, attn_lk2, subln_g, w_br_rnn,
           w_br_attn, w_out):
    f32 = np.float32
    x = np.asarray(x, f32)
    shared = {
        "w_in_t": _tile_w(np.asarray(w_in[0], f32), 64),
        "w_br_rnn_t": _tile_w(np.asarray(w_br_rnn[0], f32), 8),
        "w_br_attn_t": _tile_w(np.asarray(w_br_attn[0], f32), 8),
        "w_out_t": _tile_w(np.asarray(w_out[0], f32), 8),
        "pre_g_b": np.ascontiguousarray(np.broadcast_to(np.asarray(pre_g[0], f32), (P, D))),
        "post_g_b": np.ascontiguousarray(np.broadcast_to(np.asarray(post_g[0], f32), (P, D))),
        "conv_wt": np.ascontiguousarray(np.asarray(conv_w[0], f32).reshape(4, 8, P).transpose(2, 1, 0)),
        "vecs": np.ascontiguousarray(np.stack([
            np.asarray(conv_b[0], f32).reshape(8, P),
            np.asarray(lru_ba[0], f32).reshape(8, P),
            np.asarray(lru_bx[0], f32).reshape(8, P),
            np.asarray(lru_a[0], f32).reshape(8, P)], axis=-1).transpose(1, 0, 2)),
        "wa_bd": _blockdiag(np.asarray(lru_wa[0], f32)),
        "wx_bd": _blockdiag(np.asarray(lru_wx[0], f32)),
        "lam_qk": np.ascontiguousarray(np.broadcast_to(np.stack([
            np.asarray(attn_lq1[0], f32), np.asarray(attn_lk1[0], f32),
            np.asarray(attn_lq2[0], f32), np.asarray(attn_lk2[0], f32)], axis=0), (P, 4, 64))),
        "subln_g": np.ascontiguousarray(np.asarray(subln_g[0], f32).reshape(P, 1)),
        "ident": np.eye(P, dtype=f32).astype(ml_dtypes.bfloat16),
        "tri": np.triu(np.ones((P, P), f32)).astype(ml_dtypes.bfloat16),
        "ones_bf": np.ones((P, P), f32).astype(ml_dtypes.bfloat16),
        "ones_f": np.full((P, P), 1.0 / P, f32),
        "selb": _selb(),
        "self": _self(),
    }
    nc = build_program()
    in_maps = []
    for b in range(8):
        m = dict(shared)
        m["x"] = np.ascontiguousarray(x[b])
        in_maps.append(m)
    res = run_bass_kernel_spmd(nc, in_maps, core_ids=list(range(8)))
    return np.stack([np.asarray(r["out"], f32) for r in res.results], axis=0)
```

```python
import math
from contextlib import ExitStack

import numpy as np
import ml_dtypes

import concourse.bass as bass
import concourse.mybir as mybir
from concourse.bass_utils import run_bass_kernel_spmd

F32 = mybir.dt.float32
BF16 = mybir.dt.bfloat16
AF = mybir.ActivationFunctionType
ALU = mybir.AluOpType
AX = mybir.AxisListType

P = 128
D = 1024
S = 4096
T = 512
NCH = S // T
NORM_EPS = 1e-6
LAM_INIT = 0.8 - 0.6 * math.exp(-0.3 * 0)
NW = 6
NA = 9
NE = 2


class Counter:
    def __init__(self, name, step):
        self.name, self.step, self.val, self.sem = name, step, 0, None


class Buf:
    __slots__ = ("name", "w", "r")

    def __init__(self, name=""):
        self.name, self.w, self.r = name, None, {}


class Sched:
    ENGS = ("pe", "act", "dve", "pool", "sp")

    def __init__(self):
        self.ops = {e: [] for e in self.ENGS}
        self.ctr = {e: Counter("c_" + e, 1) for e in self.ENGS}
        self.seen = {e: {} for e in self.ENGS}
        self.counters = list(self.ctr.values())

    def new_counter(self, name, step=16):
        c = Counter(name, step)
        self.counters.append(c)
        return c

    def _waits(self, eng, reads, writes):
        need = {}
        pe_c = self.ctr["pe"]
        seen = self.seen[eng]

        def add(c, v):
            if c is pe_c and eng == "pe":
                return
            if seen.get(c, 0) >= v:
                return
            if need.get(c, 0) < v:
                need[c] = v
        for b in reads:
            if b.w is not None:
                add(*b.w)
        for b in writes:
            if b.w is not None:
                add(*b.w)
            for c, v in b.r.items():
                add(c, v)
        for c, v in need.items():
            seen[c] = v
        return list(need.items())

    def op(self, eng, fn, reads=(), writes=(), flag=True):
        waits = self._waits(eng, reads, writes)
        c = self.ctr[eng]
        if flag:
            c.val += 1
            v = c.val
        else:
            v = c.val + 1
        for b in reads:
            if b.r.get(c, 0) < v:
                b.r[c] = v
        for b in writes:
            b.w = (c, v)
            b.r = {}
        self.ops[eng].append((fn, waits, c if flag else None, v))

    def dma(self, eng, fn, counter, reads=(), writes=()):
        waits = self._waits(eng, reads, writes)
        counter.val += counter.step
        v = counter.val
        for b in reads:
            if b.r.get(counter, 0) < v:
                b.r[counter] = v
        for b in writes:
            b.w = (counter, v)
            b.r = {}
        self.ops[eng].append((fn, waits, counter, v))

    def wait_all(self, eng, cvs):
        waits = []
        for c, v in cvs:
            if v > 0 and self.seen[eng].get(c, 0) < v:
                self.seen[eng][c] = v
                waits.append((c, v))
        self.ops[eng].append((None, waits, None, 0))

    def emit(self, block):
        handles = {"pe": block.tensor, "act": block.scalar, "dve": block.vector,
                   "pool": block.gpsimd, "sp": block.sync}
        eng_ctrs = set(self.ctr.values())
        needed = {c: set() for c in eng_ctrs}
        for e in self.ENGS:
            for fn, waits, c, vid in self.ops[e]:
                for wc, wv in waits:
                    if wc in eng_ctrs:
                        needed[wc].add(wv)
        rank = {c: {v: i + 1 for i, v in enumerate(sorted(needed[c]))} for c in eng_ctrs}
        for e in self.ENGS:
            ops = self.ops[e]

            def body(eng, ops=ops):
                for fn, waits, c, vid in ops:
                    for wc, wv in waits:
                        eng.wait_ge(wc.sem, rank[wc][wv] if wc in eng_ctrs else wv)
                    if fn is None:
                        continue
                    ins = fn(eng)
                    if c is not None:
                        if c in eng_ctrs:
                            if vid in needed[c]:
                                ins.then_inc(c.sem, 1)
                        else:
                            ins.then_inc(c.sem, c.step)
            handles[e](body)


def build_program(debug=False, nopump=False):
    nc = bass.Bass("TRN2", target_bir_lowering=False)

    def din(name, shape, dt=F32):
        return nc.dram_tensor(name, list(shape), dt, kind="ExternalInput").ap()

    x_d = din("x", [S, D])
    w_in_d = din("w_in_t", [64, P, 8, 128])
    w_brr_d = din("w_br_rnn_t", [8, P, 8, 128])
    w_bra_d = din("w_br_attn_t", [8, P, 8, 128])
    w_out_d = din("w_out_t", [8, P, 8, 128])
    preg_d = din("pre_g_b", [P, D])
    postg_d = din("post_g_b", [P, D])
    convw_d = din("conv_wt", [P, 8, 4])
    vecs_d = din("vecs", [P, 8, 4])
    wabd_d = din("wa_bd", [P, 8, 128])
    wxbd_d = din("wx_bd", [P, 8, 128])
    lamqk_d = din("lam_qk", [P, 4, 64])
    sublng_d = din("subln_g", [P, 1])
    ident_d = din("ident", [P, P], BF16)
    tri_d = din("tri", [P, P], BF16)
    onesb_d = din("ones_bf", [P, P], BF16)
    onesf_d = din("ones_f", [P, P])
    selb_d = din("selb", [P, 64], BF16)
    self_d = din("self", [64, 2 * P])
    out_d = nc.dram_tensor("out", [S, D], F32, kind="ExternalOutput").ap()

    sc = Sched()
    with ExitStack() as es:
        def sb(name, shape, dt):
            return es.enter_context(nc.sbuf_tensor(name, list(shape), dt))

        Kc = sb("Kc", [P, 8, S], BF16)
        Vc = sb("Vc", [P, S // P, D], BF16)
        uT = sb("uT", [P, 8, T], BF16)
        yr = sb("yr", [P, 8, T], BF16)
        ya = sb("ya", [P, 8, T], BF16)
        mT = sb("mT", [P, 8, T], BF16)
        wring = sb("wring", [P, NW, 8, 128], BF16)
        wabd = sb("wabd", [P, 8, 128], BF16)
        wxbd = sb("wxbd", [P, 8, 128], BF16)
        arena = sb("arena", [P, NA, 512], F32)
        Et = sb("Et", [P, NE, 2, T], BF16)
        qT = sb("qT", [P, 2, T], BF16)
        xrs = sb("xrs", [P, 1, 516], F32)
        convw = sb("convw", [P, 8, 4], F32)
        vecs = sb("vecs_s", [P, 8, 4], F32)
        lamqk = sb("lamqk", [P, 4, 64], F32)
        sublng = sb("sublng", [P, 1], F32)
        ident = sb("ident_s", [P, P], BF16)
        tri = sb("tri_s", [P, P], BF16)
        onesf = sb("onesf", [P, P], F32)
        selb = sb("selb_s", [P, 64], BF16)
        self_ = sb("self_s", [64, 2 * P], F32)
        small = sb("small", [P, 8, 8], F32)
        hstate = sb("hstate", [P, 8], F32)
        xcarry = sb("xcarry", [P, 8, 3], F32)
        misc = sb("misc", [P, 16], F32)
        stat = sb("stat", [P, 64], F32)
        ps = es.enter_context(nc.psum_tensor("ps", [P, 8, 512], F32))

        K_C, K_CH, K_HBA, K_HBX, K_T0, K_T1, K_T2, K_T3 = range(8)

        B = {}

        def buf(name):
            if name not in B:
                B[name] = Buf(name)
            return B[name]

        psB = [Buf("ps%d" % i) for i in range(8)]
        arB = [Buf("ar%d" % i) for i in range(NA)]
        arC = [sc.new_counter("arc%d" % i) for i in range(NA)]
        wrB = [Buf("wr%d" % i) for i in range(NW)]
        wrC = [sc.new_counter("wrc%d" % i) for i in range(NW)]
        EB = [Buf("E%d" % i) for i in range(NE)]
        qB = [Buf("q0"), Buf("q1")]
        xrB = [Buf("xr0")]
        uTB = [Buf("uT%d" % i) for i in range(4)]
        yrB = [Buf("yr%d" % i) for i in range(8)]
        yaB = [Buf("ya%d" % i) for i in range(8)]
        mTB = [Buf("mT%d" % i) for i in range(8)]
        KB = [[Buf("K%d_%d" % (h, j)) for j in range(NCH)] for h in range(8)]
        VB = [[Buf("V%d_%d" % (h, j)) for j in range(NCH)] for h in range(8)]
        statB = [Buf("st%d" % i) for i in range(64)]
        c_const = sc.new_counter("cconst")
        c_const2 = sc.new_counter("cconst2")
        const_bufs = []
        const_bufs2 = []

        state = {"ps": 0, "wr": 0, "st": 0, "e": 0, "pslim": 8}
        ar_free = list(range(NA))

        def ar_alloc():
            assert ar_free, "arena exhausted"
            i = ar_free.pop()
            return arena[:, i, :], [arB[i]], i

        def ar_alloc2():
            for i in range(0, NA - 1, 2):
                if i in ar_free and (i + 1) in ar_free:
                    ar_free.remove(i)
                    ar_free.remove(i + 1)
                    return arena[:, i:i + 2, :], [arB[i], arB[i + 1]], i
            raise AssertionError("arena pair exhausted")

        def ar_release(*idx):
            for i in idx:
                assert i not in ar_free
                ar_free.append(i)
            ar_free.sort()

        def ps_get():
            lim = state["pslim"]
            i = state["ps"] % lim
            state["ps"] = (i + 1) % lim
            return i

        def st_get():
            i = state["st"]
            state["st"] = (i + 1) % 64
            return stat[:, i:i + 1], [statB[i]]

        def wtile(src_ap):
            s = state["wr"]
            state["wr"] = (s + 1) % NW
            sc.dma("pool", lambda e, s=s, src_ap=src_ap: e.dma_start(out=wring[:, s, :, :], in_=src_ap),
                   wrC[s], writes=[wrB[s]])
            return s

        def cdma(dst, src, b):
            sc.dma("sp", lambda e: e.dma_start(out=dst, in_=src), c_const, writes=[b])
            const_bufs.append(b)

        cdma(convw[:], convw_d[:, :, :], buf("convw"))
        cdma(vecs[:], vecs_d[:, :, :], buf("vecs"))
        cdma(lamqk[:], lamqk_d[:, :, :], buf("lamqk"))
        cdma(sublng[:], sublng_d[:, :], buf("sublng"))
        cdma(ident[:], ident_d[:, :], buf("ident"))
        cdma(tri[:], tri_d[:, :], buf("tri"))
        cdma(onesf[:], onesf_d[:, :], buf("onesf"))
        cdma(selb[:], selb_d[:, :], buf("selb"))
        cdma(self_[:], self_d[:, :], buf("self"))
        for b in const_bufs:
            b.w = (c_const, c_const.val)

        def cdma2(dst, src, b):
            sc.dma("pool", lambda e: e.dma_start(out=dst, in_=src), c_const2, writes=[b])
            const_bufs2.append(b)

        cdma2(wabd[:], wabd_d[:, :, :], buf("wabd"))
        cdma2(wxbd[:], wxbd_d[:, :, :], buf("wxbd"))
        for b in const_bufs2:
            b.w = (c_const2, c_const2.val)

        bsmall = buf("small")
        bmisc = buf("misc")
        bh = buf("hstate")
        bxc = buf("xcarry")

        sc.op("dve", lambda e: e.memset(hstate[:], 0.0), writes=[bh])
        sc.op("dve", lambda e: e.memset(xcarry[:], 0.0), writes=[bxc])

        prod, prodB, prod_i = ar_alloc()
        sc.op("dve", lambda e: e.tensor_tensor(prod[:, 0:64], lamqk[:, 0, :], lamqk[:, 1, :], ALU.mult),
              reads=[buf("lamqk")], writes=prodB)
        sc.op("dve", lambda e: e.tensor_tensor(prod[:, 64:128], lamqk[:, 2, :], lamqk[:, 3, :], ALU.mult),
              reads=[buf("lamqk")] + prodB, writes=prodB)
        sc.op("dve", lambda e: e.reduce_sum(misc[:, 1:2], prod[:, 0:64], AX.X), reads=prodB, writes=[bmisc])
        sc.op("dve", lambda e: e.reduce_sum(misc[:, 2:3], prod[:, 64:128], AX.X), reads=prodB + [bmisc], writes=[bmisc])
        sc.op("act", lambda e: e.activation(misc[:, 3:5], misc[:, 1:3], AF.Exp), reads=[bmisc], writes=[bmisc])
        sc.op("dve", lambda e: e.tensor_tensor(misc[:, 5:6], misc[:, 4:5], misc[:, 3:4], ALU.subtract),
              reads=[bmisc], writes=[bmisc])
        sc.op("dve", lambda e: e.tensor_scalar(misc[:, 0:1], misc[:, 5:6], -LAM_INIT, None, ALU.add),
              reads=[bmisc], writes=[bmisc])
        sc.op("dve", lambda e: e.tensor_scalar(misc[:, 6:7], sublng[:, 0:1], 1.0 - LAM_INIT, None, ALU.mult),
              reads=[buf("sublng"), bmisc], writes=[bmisc])
        ar_release(prod_i)
        neg_lam = misc[:, 0:1]
        subg = misc[:, 6:7]

        Lap = vecs[:, :, 3]
        t0, t1, t2, t3 = (small[:, K_T0, :], small[:, K_T1, :], small[:, K_T2, :], small[:, K_T3, :])
        bv = buf("vecs")
        sc.op("act", lambda e: e.activation(t0, Lap, AF.Exp, scale=-1.0), reads=[bv], writes=[bsmall])
        sc.op("dve", lambda e: e.tensor_scalar(t1, t0, 1.0, None, ALU.add), reads=[bsmall], writes=[bsmall])
        sc.op("act", lambda e: e.activation(t2, t1, AF.Ln), reads=[bsmall], writes=[bsmall])
        sc.op("dve", lambda e: e.tensor_scalar(t3, t1, -1.0, None, ALU.add), reads=[bsmall], writes=[bsmall])
        sc.op("dve", lambda e: e.reciprocal(t3, t3), reads=[bsmall], writes=[bsmall])
        sc.op("dve", lambda e: e.tensor_tensor(t3, t3, t0, ALU.mult), reads=[bsmall], writes=[bsmall])
        sc.op("dve", lambda e: e.tensor_tensor(t3, t3, t2, ALU.mult), reads=[bsmall], writes=[bsmall])
        sc.op("dve", lambda e: e.tensor_scalar(small[:, K_C, :], t3, -8.0, None, ALU.mult), reads=[bsmall], writes=[bsmall])
        sc.op("dve", lambda e: e.tensor_scalar(small[:, K_CH, :], t3, -4.0, None, ALU.mult), reads=[bsmall], writes=[bsmall])
        sc.op("dve", lambda e: e.tensor_scalar(small[:, K_HBA, :], vecs[:, :, 1], 0.5, None, ALU.mult),
              reads=[bv, bsmall], writes=[bsmall])
        sc.op("dve", lambda e: e.tensor_scalar(small[:, K_HBX, :], vecs[:, :, 2], 0.5, None, ALU.mult),
              reads=[bv, bsmall], writes=[bsmall])

        def proj(bank, s, rhs_t, rhs_bufs, lo=0, hi=T):
            for dt in range(8):
                sc.op("pe", lambda e, dt=dt: e.matmul(ps[:, bank, lo:hi], wring[:, s, dt, :], rhs_t[:, dt, lo:hi],
                                                      start=(dt == 0), stop=(dt == 7)),
                      reads=[wrB[s]] + rhs_bufs, writes=[psB[bank]], flag=(dt == 7))

        XB = 7
        cw = buf("convw")

        xlock = [None]

        def xb_acquire(me):
            while xlock[0] is not None and xlock[0] != me:
                yield
            xlock[0] = me

        def xb_release():
            xlock[0] = None

        def rnn_gen(j):
            xb_ = [xrB[0]]
            yield from xb_acquire("rnn")
            s_xr = wtile(w_in_d[0])
            sc.op("dve", lambda e: e.tensor_copy(xrs[:, 0, 0:3], xcarry[:, 0, :]), reads=[bxc], writes=xb_)
            proj(XB, s_xr, uT, uTB)
            yield
            sc.op("dve", lambda e: e.tensor_copy(xrs[:, 0, 3:515], ps[:, XB, :]), reads=[psB[XB]], writes=xb_)
            xb_release()
            yield
            for c in range(8):
                yield from xb_acquire("rnn")
                s_zr = wtile(w_in_d[8 + c])
                proj(XB, s_zr, uT, uTB)
                sc.op("dve", lambda e, c=c: e.tensor_copy(xcarry[:, c, :], xrs[:, 0, 512:515]), reads=xb_, writes=[bxc])
                xc, xcB, xc_i = ar_alloc()
                sc.op("dve", lambda e, xc=xc, c=c: e.tensor_scalar(
                    xc, xrs[:, 0, 3:515], convw[:, c, 3:4], vecs[:, c, 0:1], ALU.mult, ALU.add),
                    reads=xb_ + [cw, bv], writes=xcB)
                yield
                tz, tzB, tz_i = ar_alloc()
                sc.op("act", lambda e, tz=tz: e.activation(tz, ps[:, XB, :], AF.Tanh, scale=0.5),
                      reads=[psB[XB]], writes=tzB)
                sc.op("dve", lambda e, xc=xc, c=c: e.scalar_tensor_tensor(
                    xc, xrs[:, 0, 2:514], convw[:, c, 2:3], xc, ALU.mult, ALU.add), reads=xb_ + [cw] + xcB, writes=xcB)
                yield
                sc.op("dve", lambda e, tz=tz, c=c: e.scalar_tensor_tensor(
                    yr[:, c, :], tz, 1.0, ps[:, XB, :], ALU.add, ALU.mult), reads=tzB + [psB[XB]], writes=[yrB[c]])
                xb_release()
                ar_release(tz_i)
                sc.op("dve", lambda e, xc=xc, c=c: e.scalar_tensor_tensor(
                    xc, xrs[:, 0, 1:513], convw[:, c, 1:2], xc, ALU.mult, ALU.add), reads=xb_ + [cw] + xcB, writes=xcB)
                yield
                sc.op("dve", lambda e, xc=xc, c=c: e.scalar_tensor_tensor(
                    xc, xrs[:, 0, 0:512], convw[:, c, 0:1], xc, ALU.mult, ALU.add), reads=xb_ + [cw] + xcB, writes=xcB)
                xcb_, xcbB, xcb_i = ar_alloc()
                xcb = xcb_.bitcast(BF16)[:, 0:T]
                sc.op("dve", lambda e, xcb=xcb, xc=xc: e.tensor_copy(xcb, xc), reads=xcB, writes=xcbB)
                yield
                yield from xb_acquire("rnn")
                sc.op("pe", lambda e, c=c, xcb=xcb: e.matmul(ps[:, XB, :], wabd[:, c, :], xcb, start=True, stop=True),
                      reads=[buf("wabd")] + xcbB, writes=[psB[XB]])
                yield
                tr, trB, tr_i = ar_alloc()
                sc.op("act", lambda e, tr=tr, c=c: e.activation(
                    tr, ps[:, XB, :], AF.Tanh, bias=small[:, K_HBA, c:c + 1], scale=0.5),
                    reads=[psB[XB], bsmall], writes=trB)
                xb_release()
                yield
                yield from xb_acquire("rnn")
                sc.op("pe", lambda e, c=c, xcb=xcb: e.matmul(ps[:, XB, :], wxbd[:, c, :], xcb, start=True, stop=True),
                      reads=[buf("wxbd")] + xcbB, writes=[psB[XB]])
                ar_release(xcb_i)
                a_, aB, a_i = ar_alloc()
                sc.op("act", lambda e, a_=a_, tr=tr, c=c: e.activation(
                    a_, tr, AF.Exp, bias=small[:, K_CH, c:c + 1], scale=small[:, K_CH, c:c + 1]),
                    reads=trB + [bsmall], writes=aB)
                yield
                ti, tiB, ti_i = ar_alloc()
                sc.op("act", lambda e, ti=ti, c=c: e.activation(
                    ti, ps[:, XB, :], AF.Tanh, bias=small[:, K_HBX, c:c + 1], scale=0.5),
                    reads=[psB[XB], bsmall], writes=tiB)
                xb_release()
                a2, a2B, a2_i = ar_alloc()
                sc.op("act", lambda e, a2=a2, tr=tr, c=c: e.activation(
                    a2, tr, AF.Exp, bias=small[:, K_C, c:c + 1], scale=small[:, K_C, c:c + 1]),
                    reads=trB + [bsmall], writes=a2B)
                yield
                th, thB, th_i = ar_alloc()
                sc.op("act", lambda e, th=th, tr=tr, c=c: e.activation(
                    th, tr, AF.Tanh, bias=small[:, K_CH, c:c + 1], scale=small[:, K_CH, c:c + 1]),
                    reads=trB + [bsmall], writes=thB)
                ar_release(tr_i)
                sc.op("dve", lambda e, ti=ti, xc=xc: e.scalar_tensor_tensor(ti, ti, 1.0, xc, ALU.add, ALU.mult),
                      reads=tiB + xcB, writes=tiB)
                ar_release(xc_i)
                yield
                sc.op("dve", lambda e, a2=a2, th=th: e.scalar_tensor_tensor(a2, a2, 1.0, th, ALU.add, ALU.mult),
                      reads=a2B + thB, writes=a2B)
                ar_release(th_i)
                yield
                if c + 1 < 8:
                    yield from xb_acquire("rnn")
                sc.op("act", lambda e, a2=a2: e.activation(a2, a2, AF.Sqrt, scale=-0.25), reads=a2B, writes=a2B)
                if c + 1 < 8:
                    s_xr = wtile(w_in_d[c + 1])
                    sc.op("dve", lambda e, c=c: e.tensor_copy(xrs[:, 0, 0:3], xcarry[:, c + 1, :]),
                          reads=[bxc], writes=xb_)
                    proj(XB, s_xr, uT, uTB)
                yield
                if c + 1 < 8:
                    sc.op("dve", lambda e: e.tensor_copy(xrs[:, 0, 3:515], ps[:, XB, :]),
                          reads=[psB[XB]], writes=xb_)
                    xb_release()
                sc.op("dve", lambda e, ti=ti, a2=a2: e.tensor_tensor(ti, ti, a2, ALU.mult), reads=tiB + a2B, writes=tiB)
                ar_release(a2_i)
                yield
                hh, hhB, hh_i = ar_alloc()
                sc.op("dve", lambda e, hh=hh, a_=a_, ti=ti, c=c: e.tensor_tensor_scan(
                    hh, a_, ti, hstate[:, c:c + 1], ALU.mult, ALU.add), reads=aB + tiB + [bh], writes=hhB)
                ar_release(a_i, ti_i)
                sc.op("dve", lambda e, hh=hh, c=c: e.tensor_copy(hstate[:, c:c + 1], hh[:, T - 1:T]),
                      reads=hhB, writes=[bh])
                yield
                sc.op("dve", lambda e, hh=hh, c=c: e.tensor_tensor(yr[:, c, :], hh, yr[:, c, :], ALU.mult),
                      reads=hhB + [yrB[c]], writes=[yrB[c]])
                ar_release(hh_i)
                yield

        def epi_gen(h, ssb, ssbB, ss_i, o0, o0B, o0_i, o1, o1B, o1_i):
            sc.op("dve", lambda e: e.reciprocal(ssb[0:64, :], ssb[0:64, :]), reads=ssbB, writes=ssbB)
            yield
            yield from xb_acquire("epi")
            sc.op("pe", lambda e: e.matmul(ps[:, XB, :], self_[0:64, 0:P], ssb[0:64, :], start=True, stop=True),
                  reads=ssbB + [buf("self")], writes=[psB[XB]])
            yield
            sc.op("dve", lambda e: e.tensor_tensor(o0, o0, ps[:, XB, :], ALU.mult), reads=o0B + [psB[XB]], writes=o0B)
            xb_release()
            yield
            yield from xb_acquire("epi")
            sc.op("pe", lambda e: e.matmul(ps[:, XB, :], self_[0:64, P:2 * P], ssb[0:64, :], start=True, stop=True),
                  reads=ssbB + [buf("self")], writes=[psB[XB]])
            yield
            sc.op("dve", lambda e: e.tensor_tensor(o1, o1, ps[:, XB, :], ALU.mult), reads=o1B + [psB[XB]], writes=o1B)
            xb_release()
            sc.op("dve", lambda e: e.scalar_tensor_tensor(o0, o1, neg_lam, o0, ALU.mult, ALU.add),
                  reads=o0B + o1B + [bmisc], writes=o0B)
            yield
            sc.op("dve", lambda e: e.tensor_tensor(o1, o0, o0, ALU.mult), reads=o0B, writes=o1B)
            yield
            yield from xb_acquire("epi")
            sc.op("pe", lambda e: e.matmul(ps[:, XB, :], onesf[:], o1, start=True, stop=True),
                  reads=o1B + [buf("onesf")], writes=[psB[XB]])
            yield
            sc.op("dve", lambda e: e.tensor_copy(ssb, ps[:, XB, :]), reads=[psB[XB]], writes=ssbB)
            xb_release()
            ar_release(o1_i)
            sig["ln_ready"] = True
            want = sig["projk"]
            while bg["qk"] is not None and sig["projk"] == want:
                yield
            sig["ln_ready"] = False
            sc.op("act", lambda e: e.activation(ssb, ssb, AF.Ln, bias=NORM_EPS), reads=ssbB, writes=ssbB)
            sc.op("act", lambda e: e.activation(ssb, ssb, AF.Exp, scale=-0.5), reads=ssbB, writes=ssbB)
            yield
            sc.op("dve", lambda e: e.tensor_tensor(o0, o0, ssb, ALU.mult), reads=o0B + ssbB, writes=o0B)
            ar_release(ss_i)
            yield
            sc.op("dve", lambda e: e.scalar_tensor_tensor(ya[:, h, :], o0, subg, ya[:, h, :], ALU.mult, ALU.mult),
                  reads=o0B + [bmisc, yaB[h]], writes=[yaB[h]])
            ar_release(o0_i)
            yield

        bg = {"qk": None, "epi": None, "rnn": None}
        sig = {"ln_ready": False, "projk": 0}

        def pump(n=1):
            if nopump:
                return
            for _ in range(n):
                for k in ("qk", "epi", "rnn"):
                    g = bg[k]
                    if g is not None:
                        try:
                            next(g)
                        except StopIteration:
                            bg[k] = None

        def drain(k):
            while bg[k] is not None:
                for kk in ("qk", "epi", "rnn"):
                    g = bg[kk]
                    if g is not None:
                        try:
                            next(g)
                        except StopIteration:
                            bg[kk] = None

        for j in range(NCH):
            tok0 = j * T
            state["pslim"] = 8
            pg, pgB, pgi = ar_alloc2()
            pgf = pg.rearrange("p a b -> p (a b)")
            sc.dma("sp", lambda e, pgf=pgf: e.dma_start(out=pgf, in_=preg_d[:, :]), arC[pgi], writes=pgB)
            s1 = {}

            def s1_front(tt):
                r0 = tok0 + tt * P
                xt, xtB, xi = ar_alloc2()
                xf = xt.rearrange("p a b -> p (a b)")
                sc.dma("sp", lambda e: e.dma_start(out=xf, in_=x_d[r0:r0 + P, :]), arC[xi], writes=xtB)
                ub, ubB, ub_i = ar_alloc()
                ubb = ub.bitcast(BF16)
                ss, ssB = st_get()
                sc.op("act", lambda e: e.activation(ubb, xf, AF.Square, accum_out=ss), reads=xtB, writes=ubB + ssB)
                sd, sdB = st_get()
                sc.op("act", lambda e: e.activation(sd, ss, AF.Sqrt, bias=NORM_EPS, scale=1.0 / D),
                      reads=ssB, writes=sdB)
                rs, rsB = st_get()
                sc.op("dve", lambda e: e.reciprocal(rs, sd), reads=sdB, writes=rsB)
                sc.op("dve", lambda e, pgf=pgf: e.scalar_tensor_tensor(ubb, xf, rs, pgf, ALU.mult, ALU.mult),
                      reads=xtB + rsB + pgB + ubB, writes=ubB)
                s1[tt] = (xi, ubb, ubB, ub_i)

            def s1_back(tt):
                xi, ubb, ubB, ub_i = s1.pop(tt)
                ar_release(xi, xi + 1)
                bank = ps_get()
                pst = ps[:, bank, :].bitcast(BF16)
                for dt in range(8):
                    sc.op("pe", lambda e, dt=dt: e.transpose(
                        pst[:, dt * P:(dt + 1) * P], ubb[:, dt * P:(dt + 1) * P], ident[:]),
                        reads=ubB + [buf("ident")], writes=[psB[bank]], flag=(dt == 7))
                ar_release(ub_i)
                sc.op("dve", lambda e: e.tensor_copy(
                    uT[:, :, tt * P:(tt + 1) * P], pst.rearrange("p (a b) -> p a b", a=8)),
                    reads=[psB[bank]], writes=[uTB[tt]])

            s1_front(0)
            s1_front(1)
            s1_back(0)
            s1_front(2)
            s1_back(1)
            s1_front(3)
            s1_back(2)
            s1_back(3)
            ar_release(pgi, pgi + 1)

            bg["rnn"] = rnn_gen(j)
            n_steps = 16 + 8 * (4 * j + 4)
            npump = max(1, -(-(8 * 17) // n_steps))
            state["pslim"] = 6
            state["ps"] = 0

            for h in range(8):
                s_za = wtile(w_in_d[40 + h])
                b_za = ps_get()
                proj(b_za, s_za, uT, uTB)
                tza, tzaB, tza_i = ar_alloc()
                sc.op("act", lambda e, tza=tza, b_za=b_za: e.activation(tza, ps[:, b_za, :], AF.Tanh, scale=0.5),
                      reads=[psB[b_za]], writes=tzaB)
                sc.op("dve", lambda e, tza=tza, b_za=b_za, h=h: e.scalar_tensor_tensor(
                    ya[:, h, :], tza, 1.0, ps[:, b_za, :], ALU.add, ALU.mult),
                    reads=tzaB + [psB[b_za]], writes=[yaB[h]])
                ar_release(tza_i)
                pump(npump)
            for h in range(8):
                s_v = wtile(w_in_d[32 + h])
                b_v = ps_get()
                for tt in range(4):
                    for dt in range(8):
                        sc.op("pe", lambda e, b_v=b_v, tt=tt, dt=dt, s_v=s_v: e.matmul(
                            ps[:, b_v, tt * P:(tt + 1) * P], uT[:, dt, tt * P:(tt + 1) * P], wring[:, s_v, dt, :],
                            start=(dt == 0), stop=(dt == 7)),
                            reads=[wrB[s_v], uTB[tt]], writes=[psB[b_v]], flag=(dt == 7))
                sc.op("dve", lambda e, b_v=b_v, h=h, j=j: e.tensor_copy(
                    Vc[:, 4 * j:4 * j + 4, h * P:(h + 1) * P], ps[:, b_v, :].rearrange("p (a b) -> p a b", a=4)),
                    reads=[psB[b_v]], writes=[VB[h][j]])
                pump(npump)
            nkt = 4 * j + 4

            def qk_inline(h):
                s_q = wtile(w_in_d[16 + h])
                s_k = wtile(w_in_d[24 + h])
                proj(0, s_q, uT, uTB)
                proj(2, s_k, uT, uTB)
                qq = h % 2
                sc.op("dve", lambda e: e.tensor_copy(qT[:, qq, :], ps[:, 0, :]), reads=[psB[0]], writes=[qB[qq]])
                sc.op("dve", lambda e, tok0=tok0: e.tensor_copy(Kc[:, h, tok0:tok0 + T], ps[:, 2, :]),
                      reads=[psB[2]], writes=[KB[h][j]])

            def qk_gen(h):
                qq = h % 2
                yield from xb_acquire("qk")
                s_q = wtile(w_in_d[16 + h])
                proj(XB, s_q, uT, uTB)
                yield
                sc.op("dve", lambda e: e.tensor_copy(qT[:, qq, :], ps[:, XB, :]), reads=[psB[XB]], writes=[qB[qq]])
                xb_release()
                yield
                while bg["epi"] is not None and not sig["ln_ready"]:
                    yield
                yield from xb_acquire("qk")
                s_k = wtile(w_in_d[24 + h])
                proj(XB, s_k, uT, uTB)
                sig["projk"] += 1
                yield
                sc.op("dve", lambda e, tok0=tok0: e.tensor_copy(Kc[:, h, tok0:tok0 + T], ps[:, XB, :]),
                      reads=[psB[XB]], writes=[KB[h][j]])
                xb_release()
                yield

            def qk(h, kt, gi):
                i = kt - 4 * j
                lo = P * i if i > 0 else 0
                pr = gi % 2
                qq = h % 2
                for c2 in range(2):
                    bank = 2 * pr + c2
                    sc.op("pe", lambda e, bank=bank, c2=c2: e.matmul(
                        ps[:, bank, lo:T], Kc[64 * c2:64 * c2 + 64, h, kt * P:(kt + 1) * P],
                        qT[64 * c2:64 * c2 + 64, qq, lo:T], start=True, stop=True),
                        reads=[KB[h][kt // 4], qB[qq]], writes=[psB[bank]])

            steps = [(h, kt) for h in range(8) for kt in range(nkt)]
            qk_inline(0)
            qk(0, 0, 0)
            for gi, (h, kt) in enumerate(steps):
                if kt == 0 and h + 1 < 8:
                    bg["qk"] = qk_gen(h + 1)
                if gi + 1 < len(steps):
                    h2, kt2 = steps[gi + 1]
                    if kt2 == 0:
                        drain("qk")
                    qk(h2, kt2, gi + 1)
                i = kt - 4 * j
                lo = P * i if i > 0 else 0
                pr = gi % 2
                ei = state["e"]
                state["e"] = (ei + 1) % NE
                sc.op("act", lambda e, ei=ei, pr=pr, lo=lo: e.activation(
                    Et[:, ei, :, lo:T], ps[:, 2 * pr:2 * pr + 2, lo:T], AF.Exp, scale=0.125),
                    reads=[psB[2 * pr], psB[2 * pr + 1]], writes=[EB[ei]])
                if i >= 0:
                    for c2 in range(2):
                        sc.op("dve", lambda e, ei=ei, c2=c2, lo=lo: e.tensor_tensor(
                            Et[:, ei, c2, lo:lo + P], Et[:, ei, c2, lo:lo + P], tri[:], ALU.mult),
                            reads=[EB[ei], buf("tri")], writes=[EB[ei]])
                first, last = (kt == 0), (kt == nkt - 1)
                for c2 in range(2):
                    sc.op("pe", lambda e, ei=ei, c2=c2, lo=lo, first=first, last=last: e.matmul(
                        ps[32 * c2:32 * c2 + 32, 6, lo:T], selb[:, 32 * c2:32 * c2 + 32], Et[:, ei, c2, lo:T],
                        start=first, stop=last),
                        reads=[EB[ei], buf("selb")], writes=[psB[6]], flag=False)
                for c2 in range(2):
                    sc.op("pe", lambda e, ei=ei, c2=c2, kt=kt, lo=lo, first=first, last=last, h=h: e.matmul(
                        ps[:, 4 + c2, lo:T], Vc[:, kt, h * P:(h + 1) * P], Et[:, ei, c2, lo:T],
                        start=first, stop=last),
                        reads=[EB[ei], VB[h][kt // 4]], writes=[psB[4 + c2]], flag=(c2 == 1))
                pump(npump)
                if last:
                    drain("epi")
                    ssb, ssbB, ss_i = ar_alloc()
                    o0, o0B, o0_i = ar_alloc()
                    o1, o1B, o1_i = ar_alloc()
                    sc.op("dve", lambda e, ssb=ssb: e.tensor_copy(ssb[0:64, :], ps[0:64, 6, :]),
                          reads=[psB[6]], writes=ssbB)
                    sc.op("act", lambda e, o0=o0: e.activation(o0, ps[:, 4, :], AF.Copy), reads=[psB[4]], writes=o0B)
                    sc.op("dve", lambda e, o1=o1: e.tensor_copy(o1, ps[:, 5, :]), reads=[psB[5]], writes=o1B)
                    bg["epi"] = epi_gen(h, ssb, ssbB, ss_i, o0, o0B, o0_i, o1, o1B, o1_i)
            drain("rnn")
            state["pslim"] = 7
            state["ps"] = 0

            s5x = {}

            def s5_load(tt):
                r0 = tok0 + tt * P
                xt, xtB, xi = ar_alloc2()
                xf = xt.rearrange("p a b -> p (a b)")
                sc.dma("sp", lambda e: e.dma_start(out=xf, in_=x_d[r0:r0 + P, :]), arC[xi], writes=xtB)
                s5x[tt] = (xt, xf, xtB, xi)

            for dt in range(8):
                s_gr = wtile(w_in_d[48 + dt])
                s_ga = wtile(w_in_d[56 + dt])
                s_br = wtile(w_brr_d[dt])
                s_ba = wtile(w_bra_d[dt])
                b_gr = ps_get()
                proj(b_gr, s_gr, uT, uTB)
                pump(2)
                b_ga = ps_get()
                proj(b_ga, s_ga, uT, uTB)
                pump(2)
                b_pr = ps_get()
                proj(b_pr, s_br, yr, yrB)
                if dt == 0:
                    drain("epi")
                    pg, pgB, pgi = ar_alloc2()
                    sc.dma("sp", lambda e, pg=pg: e.dma_start(out=pg.rearrange("p a b -> p (a b)"), in_=postg_d[:, :]),
                           arC[pgi], writes=pgB)
                    s5_load(0)
                    s5_load(1)
                b_pa = ps_get()
                proj(b_pa, s_ba, ya, yaB)
                tg, tgB, tg_i = ar_alloc()
                tg2, tg2B, tg2_i = ar_alloc()
                sc.op("act", lambda e, tg=tg, b_gr=b_gr: e.activation(tg, ps[:, b_gr, :], AF.Tanh, scale=0.5),
                      reads=[psB[b_gr]], writes=tgB)
                sc.op("act", lambda e, tg2=tg2, b_ga=b_ga: e.activation(tg2, ps[:, b_ga, :], AF.Tanh, scale=0.5),
                      reads=[psB[b_ga]], writes=tg2B)
                sc.op("dve", lambda e, tg=tg, b_pr=b_pr: e.scalar_tensor_tensor(
                    tg, tg, 1.0, ps[:, b_pr, :], ALU.add, ALU.mult), reads=tgB + [psB[b_pr]], writes=tgB)
                sc.op("dve", lambda e, tg2=tg2, b_pa=b_pa: e.scalar_tensor_tensor(
                    tg2, tg2, 1.0, ps[:, b_pa, :], ALU.add, ALU.mult), reads=tg2B + [psB[b_pa]], writes=tg2B)
                sc.op("dve", lambda e, tg=tg, tg2=tg2, dt=dt: e.tensor_tensor(mT[:, dt, :], tg, tg2, ALU.add),
                      reads=tgB + tg2B, writes=[mTB[dt]])
                ar_release(tg_i, tg2_i)
            state["pslim"] = 8

            for ft in range(8):
                s_o = wtile(w_out_d[ft])
                for tt in range(4):
                    bank = 2 * tt + ft // 4
                    c0 = (ft % 4) * P
                    for dt in range(8):
                        sc.op("pe", lambda e, bank=bank, c0=c0, dt=dt, tt=tt, s_o=s_o: e.matmul(
                            ps[:, bank, c0:c0 + P], mT[:, dt, tt * P:(tt + 1) * P], wring[:, s_o, dt, :],
                            start=(dt == 0), stop=(dt == 7)),
                            reads=[mTB[dt], wrB[s_o]], writes=[psB[bank]], flag=(dt == 7))
            junk, junkB, junk_i = ar_alloc()
            junkb = junk.bitcast(BF16).rearrange("p (a b) -> p a b", a=2)
            s5r = {}

            def s5_front(tt):
                yps = ps[:, 2 * tt:2 * tt + 2, :]
                ypsB = [psB[2 * tt], psB[2 * tt + 1]]
                ss, ssB = st_get()
                sc.op("act", lambda e, junkb=junkb: e.activation(junkb, yps, AF.Square, accum_out=ss), reads=ypsB, writes=junkB + ssB)
                sd, sdB = st_get()
                sc.op("act", lambda e: e.activation(sd, ss, AF.Sqrt, bias=16.0 * NORM_EPS, scale=1.0 / D),
                      reads=ssB, writes=sdB)
                rs, rsB = st_get()
                sc.op("dve", lambda e: e.reciprocal(rs, sd), reads=sdB, writes=rsB)
                s5r[tt] = (rs, rsB)

            def s5_back(tt):
                r0 = tok0 + tt * P
                yps = ps[:, 2 * tt:2 * tt + 2, :]
                ypsB = [psB[2 * tt], psB[2 * tt + 1]]
                rs, rsB = s5r.pop(tt)
                xt, xf, xtB, xi = s5x.pop(tt)
                sc.op("dve", lambda e, pg=pg: e.tensor_tensor(yps, yps, pg, ALU.mult), reads=ypsB + pgB, writes=ypsB)
                sc.op("dve", lambda e: e.scalar_tensor_tensor(xt, yps, rs, xt, ALU.mult, ALU.add),
                      reads=ypsB + rsB + xtB, writes=xtB)
                sc.dma("sp", lambda e: e.dma_start(out=out_d[r0:r0 + P, :], in_=xf), arC[xi], reads=xtB)
                ar_release(xi, xi + 1)

            s5_front(0)
            s5_load(2)
            s5_front(1)
            s5_back(0)
            s5_load(3)
            s5_front(2)
            s5_back(1)
            s5_front(3)
            s5_back(2)
            s5_back(3)
            ar_release(junk_i)
            ar_release(pgi, pgi + 1)

        if debug:
            c_dbg = sc.new_counter("cdbg")
            def dump(name, t_ap, shape, dt, bufs):
                d = nc.dram_tensor(name, list(shape), dt, kind="ExternalOutput").ap()
                sc.dma("sp", lambda e: e.dma_start(out=d, in_=t_ap), c_dbg, reads=bufs)
            dump("dbg_uT", uT[:], [P, 8, T], BF16, uTB)
            dump("dbg_yr", yr[:], [P, 8, T], BF16, yrB)
            dump("dbg_ya", ya[:], [P, 8, T], BF16, yaB)
            dump("dbg_mT", mT[:], [P, 8, T], BF16, mTB)
            dump("dbg_Kc", Kc[:], [P, 8, S], BF16, [b for l in KB for b in l])
            dump("dbg_Vc", Vc[:], [P, S // P, D], BF16, [b for l in VB for b in l])
            dump("dbg_small", small[:], [P, 8, 8], F32, [bsmall])
            dump("dbg_misc", misc[:], [P, 16], F32, [bmisc])
            dump("dbg_hstate", hstate[:], [P, 8], F32, [bh])
            dump("dbg_stat", stat[:], [P, 64], F32, statB)
            sc.wait_all("sp", [(c_dbg, c_dbg.val)])
        sc.wait_all("sp", [(c, c.val) for c in arC])

        for c in sc.counters:
            c.sem = es.enter_context(nc.semaphore(c.name))
        block = es.enter_context(nc.Block())
        sc.emit(block)
    return nc


def _tile_w(w, nft):
    return np.ascontiguousarray(w.reshape(8, P, nft, 128).transpose(2, 1, 0, 3))


def _selb():
    return np.ones((P, 64), np.float32).astype(ml_dtypes.bfloat16)


def _self():
    m = np.zeros((64, 2 * P), np.float32)
    m[0, 0:P] = 1.0
    m[32, P:2 * P] = 1.0
    return m


def _blockdiag(w):
    out = np.zeros((P, 8, P), np.float32)
    for g in range(16):
        c, o = g // 2, (g % 2) * 64
        out[o:o + 64, c, o:o + 64] = w[g]
    return out


def kernel(x, pre_g, post_g, w_in, conv_w, conv_b, lru_wa, lru_ba, lru_wx, lru_bx,
           lru_a, attn_lq1, attn_lk1, attn_lq2, attn_lk2, subln_g, w_br_rnn,
           w_br_attn, w_out):
    f32 = np.float32
    x = np.asarray(x, f32)
    shared = {
        "w_in_t": _tile_w(np.asarray(w_in[0], f32), 64),
        "w_br_rnn_t": _tile_w(np.asarray(w_br_rnn[0], f32), 8),
        "w_br_attn_t": _tile_w(np.asarray(w_br_attn[0], f32), 8),
        "w_out_t": _tile_w(np.asarray(w_out[0], f32), 8),
        "pre_g_b": np.ascontiguousarray(np.broadcast_to(np.asarray(pre_g[0], f32), (P, D))),
        "post_g_b": np.ascontiguousarray(np.broadcast_to(np.asarray(post_g[0], f32), (P, D))),
        "conv_wt": np.ascontiguousarray(np.asarray(conv_w[0], f32).reshape(4, 8, P).transpose(2, 1, 0)),
        "vecs": np.ascontiguousarray(np.stack([
            np.asarray(conv_b[0], f32).reshape(8, P),
            np.asarray(lru_ba[0], f32).reshape(8, P),
            np.asarray(lru_bx[0], f32).reshape(8, P),
            np.asarray(lru_a[0], f32).reshape(8, P)], axis=-1).transpose(1, 0, 2)),
        "wa_bd": _blockdiag(np.asarray(lru_wa[0], f32)),
        "wx_bd": _blockdiag(np.asarray(lru_wx[0], f32)),
        "lam_qk": np.ascontiguousarray(np.broadcast_to(np.stack([
            np.asarray(attn_lq1[0], f32), np.asarray(attn_lk1[0], f32),
            np.asarray(attn_lq2[0], f32), np.asarray(attn_lk2[0], f32)], axis=0), (P, 4, 64))),
        "subln_g": np.ascontiguousarray(np.asarray(subln_g[0], f32).reshape(P, 1)),
        "ident": np.eye(P, dtype=f32).astype(ml_dtypes.bfloat16),
        "tri": np.triu(np.ones((P, P), f32)).astype(ml_dtypes.bfloat16),
        "ones_bf": np.ones((P, P), f32).astype(ml_dtypes.bfloat16),
        "ones_f": np.full((P, P), 1.0 / P, f32),
        "selb": _selb(),
        "self": _self(),
    }
    nc = build_program()
    in_maps = []
    for b in range(8):
        m = dict(shared)
        m["x"] = np.ascontiguousarray(x[b])
        in_maps.append(m)
    res = run_bass_kernel_spmd(nc, in_maps, core_ids=list(range(8)))
    return np.stack([np.asarray(r["out"], f32) for r in res.results], axis=0)
```

```python
import math
from contextlib import ExitStack

import numpy as np
import ml_dtypes

import concourse.bass as bass
import concourse.mybir as mybir
from concourse.bass_utils import run_bass_kernel_spmd

F32 = mybir.dt.float32
BF16 = mybir.dt.bfloat16
AF = mybir.ActivationFunctionType
ALU = mybir.AluOpType
AX = mybir.AxisListType

P = 128
D = 1024
S = 4096
T = 512
NCH = S // T
NORM_EPS = 1e-6
LAM_INIT = 0.8 - 0.6 * math.exp(-0.3 * 0)
NW = 6
NA = 9
NE = 2


class Counter:
    def __init__(self, name, step):
        self.name, self.step, self.val, self.sem = name, step, 0, None


class Buf:
    __slots__ = ("name", "w", "r")

    def __init__(self, name=""):
        self.name, self.w, self.r = name, None, {}


class Sched:
    ENGS = ("pe", "act", "dve", "pool", "sp")

    def __init__(self):
        self.ops = {e: [] for e in self.ENGS}
        self.ctr = {e: Counter("c_" + e, 1) for e in self.ENGS}
        self.seen = {e: {} for e in self.ENGS}
        self.counters = list(self.ctr.values())

    def new_counter(self, name, step=16):
        c = Counter(name, step)
        self.counters.append(c)
        return c

    def _waits(self, eng, reads, writes):
        need = {}
        pe_c = self.ctr["pe"]
        seen = self.seen[eng]

        def add(c, v):
            if c is pe_c and eng == "pe":
                return
            if seen.get(c, 0) >= v:
                return
            if need.get(c, 0) < v:
                need[c] = v
        for b in reads:
            if b.w is not None:
                add(*b.w)
        for b in writes:
            if b.w is not None:
                add(*b.w)
            for c, v in b.r.items():
                add(c, v)
        for c, v in need.items():
            seen[c] = v
        return list(need.items())

    def op(self, eng, fn, reads=(), writes=(), flag=True):
        waits = self._waits(eng, reads, writes)
        c = self.ctr[eng]
        if flag:
            c.val += 1
            v = c.val
        else:
            v = c.val + 1
        for b in reads:
            if b.r.get(c, 0) < v:
                b.r[c] = v
        for b in writes:
            b.w = (c, v)
            b.r = {}
        self.ops[eng].append((fn, waits, c if flag else None, v))

    def dma(self, eng, fn, counter, reads=(), writes=()):
        waits = self._waits(eng, reads, writes)
        counter.val += counter.step
        v = counter.val
        for b in reads:
            if b.r.get(counter, 0) < v:
                b.r[counter] = v
        for b in writes:
            b.w = (counter, v)
            b.r = {}
        self.ops[eng].append((fn, waits, counter, v))

    def wait_all(self, eng, cvs):
        waits = []
        for c, v in cvs:
            if v > 0 and self.seen[eng].get(c, 0) < v:
                self.seen[eng][c] = v
                waits.append((c, v))
        self.ops[eng].append((None, waits, None, 0))

    def emit(self, block):
        handles = {"pe": block.tensor, "act": block.scalar, "dve": block.vector,
                   "pool": block.gpsimd, "sp": block.sync}
        eng_ctrs = set(self.ctr.values())
        needed = {c: set() for c in eng_ctrs}
        for e in self.ENGS:
            for fn, waits, c, vid in self.ops[e]:
                for wc, wv in waits:
                    if wc in eng_ctrs:
                        needed[wc].add(wv)
        rank = {c: {v: i + 1 for i, v in enumerate(sorted(needed[c]))} for c in eng_ctrs}
        for e in self.ENGS:
            ops = self.ops[e]

            def body(eng, ops=ops):
                for fn, waits, c, vid in ops:
                    for wc, wv in waits:
                        eng.wait_ge(wc.sem, rank[wc][wv] if wc in eng_ctrs else wv)
                    if fn is None:
                        continue
                    ins = fn(eng)
                    if c is not None:
                        if c in eng_ctrs:
                            if vid in needed[c]:
                                ins.then_inc(c.sem, 1)
                        else:
                            ins.then_inc(c.sem, c.step)
            handles[e](body)


def build_program(debug=False, nopump=False):
    nc = bass.Bass("TRN2", target_bir_lowering=False)

    def din(name, shape, dt=F32):
        return nc.dram_tensor(name, list(shape), dt, kind="ExternalInput").ap()

    x_d = din("x", [S, D])
    w_in_d = din("w_in_t", [64, P, 8, 128])
    w_brr_d = din("w_br_rnn_t", [8, P, 8, 128])
    w_bra_d = din("w_br_attn_t", [8, P, 8, 128])
    w_out_d = din("w_out_t", [8, P, 8, 128])
    preg_d = din("pre_g_b", [P, D])
    postg_d = din("post_g_b", [P, D])
    convw_d = din("conv_wt", [P, 8, 4])
    vecs_d = din("vecs", [P, 8, 4])
    wabd_d = din("wa_bd", [P, 8, 128])
    wxbd_d = din("wx_bd", [P, 8, 128])
    lamqk_d = din("lam_qk", [P, 4, 64])
    sublng_d = din("subln_g", [P, 1])
    ident_d = din("ident", [P, P], BF16)
    tri_d = din("tri", [P, P], BF16)
    onesb_d = din("ones_bf", [P, P], BF16)
    onesf_d = din("ones_f", [P, P])
    selb_d = din("selb", [P, 64], BF16)
    self_d = din("self", [64, 2 * P])
    out_d = nc.dram_tensor("out", [S, D], F32, kind="ExternalOutput").ap()

    sc = Sched()
    with ExitStack() as es:
        def sb(name, shape, dt):
            return es.enter_context(nc.sbuf_tensor(name, list(shape), dt))

        Kc = sb("Kc", [P, 8, S], BF16)
        Vc = sb("Vc", [P, S // P, D], BF16)
        uT = sb("uT", [P, 8, T], BF16)
        yr = sb("yr", [P, 8, T], BF16)
        ya = sb("ya", [P, 8, T], BF16)
        mT = sb("mT", [P, 8, T], BF16)
        wring = sb("wring", [P, NW, 8, 128], BF16)
        wabd = sb("wabd", [P, 8, 128], BF16)
        wxbd = sb("wxbd", [P, 8, 128], BF16)
        arena = sb("arena", [P, NA, 512], F32)
        Et = sb("Et", [P, NE, 2, T], BF16)
        qT = sb("qT", [P, 2, T], BF16)
        xrs = sb("xrs", [P, 1, 516], F32)
        convw = sb("convw", [P, 8, 4], F32)
        vecs = sb("vecs_s", [P, 8, 4], F32)
        lamqk = sb("lamqk", [P, 4, 64], F32)
        sublng = sb("sublng", [P, 1], F32)
        ident = sb("ident_s", [P, P], BF16)
        tri = sb("tri_s", [P, P], BF16)
        onesf = sb("onesf", [P, P], F32)
        selb = sb("selb_s", [P, 64], BF16)
        self_ = sb("self_s", [64, 2 * P], F32)
        small = sb("small", [P, 8, 8], F32)
        hstate = sb("hstate", [P, 8], F32)
        xcarry = sb("xcarry", [P, 8, 3], F32)
        misc = sb("misc", [P, 16], F32)
        stat = sb("stat", [P, 64], F32)
        ps = es.enter_context(nc.psum_tensor("ps", [P, 8, 512], F32))

        K_C, K_CH, K_HBA, K_HBX, K_T0, K_T1, K_T2, K_T3 = range(8)

        B = {}

        def buf(name):
            if name not in B:
                B[name] = Buf(name)
            return B[name]

        psB = [Buf("ps%d" % i) for i in range(8)]
        arB = [Buf("ar%d" % i) for i in range(NA)]
        arC = [sc.new_counter("arc%d" % i) for i in range(NA)]
        wrB = [Buf("wr%d" % i) for i in range(NW)]
        wrC = [sc.new_counter("wrc%d" % i) for i in range(NW)]
        EB = [Buf("E%d" % i) for i in range(NE)]
        qB = [Buf("q0"), Buf("q1")]
        xrB = [Buf("xr0")]
        uTB = [Buf("uT%d" % i) for i in range(4)]
        yrB = [Buf("yr%d" % i) for i in range(8)]
        yaB = [Buf("ya%d" % i) for i in range(8)]
        mTB = [Buf("mT%d" % i) for i in range(8)]
        KB = [[Buf("K%d_%d" % (h, j)) for j in range(NCH)] for h in range(8)]
        VB = [[Buf("V%d_%d" % (h, j)) for j in range(NCH)] for h in range(8)]
        statB = [Buf("st%d" % i) for i in range(64)]
        c_const = sc.new_counter("cconst")
        c_const2 = sc.new_counter("cconst2")
        const_bufs = []
        const_bufs2 = []

        state = {"ps": 0, "wr": 0, "st": 0, "e": 0, "pslim": 8}
        ar_free = list(range(NA))

        def ar_alloc():
            assert ar_free, "arena exhausted"
            i = ar_free.pop()
            return arena[:, i, :], [arB[i]], i

        def ar_alloc2():
            for i in range(0, NA - 1, 2):
                if i in ar_free and (i + 1) in ar_free:
                    ar_free.remove(i)
                    ar_free.remove(i + 1)
                    return arena[:, i:i + 2, :], [arB[i], arB[i + 1]], i
            raise AssertionError("arena pair exhausted")

        def ar_release(*idx):
            for i in idx:
                assert i not in ar_free
                ar_free.append(i)
            ar_free.sort()

        def ps_get():
            lim = state["pslim"]
            i = state["ps"] % lim
            state["ps"] = (i + 1) % lim
            return i

        def st_get():
            i = state["st"]
            state["st"] = (i + 1) % 64
            return stat[:, i:i + 1], [statB[i]]

        def wtile(src_ap):
            s = state["wr"]
            state["wr"] = (s + 1) % NW
            sc.dma("pool", lambda e, s=s, src_ap=src_ap: e.dma_start(out=wring[:, s, :, :], in_=src_ap),
                   wrC[s], writes=[wrB[s]])
            return s

        def cdma(dst, src, b):
            sc.dma("sp", lambda e: e.dma_start(out=dst, in_=src), c_const, writes=[b])
            const_bufs.append(b)

        cdma(convw[:], convw_d[:, :, :], buf("convw"))
        cdma(vecs[:], vecs_d[:, :, :], buf("vecs"))
        cdma(lamqk[:], lamqk_d[:, :, :], buf("lamqk"))
        cdma(sublng[:], sublng_d[:, :], buf("sublng"))
        cdma(ident[:], ident_d[:, :], buf("ident"))
        cdma(tri[:], tri_d[:, :], buf("tri"))
        cdma(onesf[:], onesf_d[:, :], buf("onesf"))
        cdma(selb[:], selb_d[:, :], buf("selb"))
        cdma(self_[:], self_d[:, :], buf("self"))
        for b in const_bufs:
            b.w = (c_const, c_const.val)

        def cdma2(dst, src, b):
            sc.dma("pool", lambda e: e.dma_start(out=dst, in_=src), c_const2, writes=[b])
            const_bufs2.append(b)

        cdma2(wabd[:], wabd_d[:, :, :], buf("wabd"))
        cdma2(wxbd[:], wxbd_d[:, :, :], buf("wxbd"))
        for b in const_bufs2:
            b.w = (c_const2, c_const2.val)

        bsmall = buf("small")
        bmisc = buf("misc")
        bh = buf("hstate")
        bxc = buf("xcarry")

        sc.op("dve", lambda e: e.memset(hstate[:], 0.0), writes=[bh])
        sc.op("dve", lambda e: e.memset(xcarry[:], 0.0), writes=[bxc])

        prod, prodB, prod_i = ar_alloc()
        sc.op("dve", lambda e: e.tensor_tensor(prod[:, 0:64], lamqk[:, 0, :], lamqk[:, 1, :], ALU.mult),
              reads=[buf("lamqk")], writes=prodB)
        sc.op("dve", lambda e: e.tensor_tensor(prod[:, 64:128], lamqk[:, 2, :], lamqk[:, 3, :], ALU.mult),
              reads=[buf("lamqk")] + prodB, writes=prodB)
        sc.op("dve", lambda e: e.reduce_sum(misc[:, 1:2], prod[:, 0:64], AX.X), reads=prodB, writes=[bmisc])
        sc.op("dve", lambda e: e.reduce_sum(misc[:, 2:3], prod[:, 64:128], AX.X), reads=prodB + [bmisc], writes=[bmisc])
        sc.op("act", lambda e: e.activation(misc[:, 3:5], misc[:, 1:3], AF.Exp), reads=[bmisc], writes=[bmisc])
        sc.op("dve", lambda e: e.tensor_tensor(misc[:, 5:6], misc[:, 4:5], misc[:, 3:4], ALU.subtract),
              reads=[bmisc], writes=[bmisc])
        sc.op("dve", lambda e: e.tensor_scalar(misc[:, 0:1], misc[:, 5:6], -LAM_INIT, None, ALU.add),
              reads=[bmisc], writes=[bmisc])
        sc.op("dve", lambda e: e.tensor_scalar(misc[:, 6:7], sublng[:, 0:1], 1.0 - LAM_INIT, None, ALU.mult),
              reads=[buf("sublng"), bmisc], writes=[bmisc])
        ar_release(prod_i)
        neg_lam = misc[:, 0:1]
        subg = misc[:, 6:7]

        Lap = vecs[:, :, 3]
        t0, t1, t2, t3 = (small[:, K_T0, :], small[:, K_T1, :], small[:, K_T2, :], small[:, K_T3, :])
        bv = buf("vecs")
        sc.op("act", lambda e: e.activation(t0, Lap, AF.Exp, scale=-1.0), reads=[bv], writes=[bsmall])
        sc.op("dve", lambda e: e.tensor_scalar(t1, t0, 1.0, None, ALU.add), reads=[bsmall], writes=[bsmall])
        sc.op("act", lambda e: e.activation(t2, t1, AF.Ln), reads=[bsmall], writes=[bsmall])
        sc.op("dve", lambda e: e.tensor_scalar(t3, t1, -1.0, None, ALU.add), reads=[bsmall], writes=[bsmall])
        sc.op("dve", lambda e: e.reciprocal(t3, t3), reads=[bsmall], writes=[bsmall])
        sc.op("dve", lambda e: e.tensor_tensor(t3, t3, t0, ALU.mult), reads=[bsmall], writes=[bsmall])
        sc.op("dve", lambda e: e.tensor_tensor(t3, t3, t2, ALU.mult), reads=[bsmall], writes=[bsmall])
        sc.op("dve", lambda e: e.tensor_scalar(small[:, K_C, :], t3, -8.0, None, ALU.mult), reads=[bsmall], writes=[bsmall])
        sc.op("dve", lambda e: e.tensor_scalar(small[:, K_CH, :], t3, -4.0, None, ALU.mult), reads=[bsmall], writes=[bsmall])
        sc.op("dve", lambda e: e.tensor_scalar(small[:, K_HBA, :], vecs[:, :, 1], 0.5, None, ALU.mult),
              reads=[bv, bsmall], writes=[bsmall])
        sc.op("dve", lambda e: e.tensor_scalar(small[:, K_HBX, :], vecs[:, :, 2], 0.5, None, ALU.mult),
              reads=[bv, bsmall], writes=[bsmall])

        def proj(bank, s, rhs_t, rhs_bufs, lo=0, hi=T):
            for dt in range(8):
                sc.op("pe", lambda e, dt=dt: e.matmul(ps[:, bank, lo:hi], wring[:, s, dt, :], rhs_t[:, dt, lo:hi],
                                                      start=(dt == 0), stop=(dt == 7)),
                      reads=[wrB[s]] + rhs_bufs, writes=[psB[bank]], flag=(dt == 7))

        XB = 7
        cw = buf("convw")

        xlock = [None]

        def xb_acquire(me):
            while xlock[0] is not None and xlock[0] != me:
                yield
            xlock[0] = me

        def xb_release():
            xlock[0] = None

        def rnn_gen(cs, alloc, release, xr, xb_, tag):
            yield from xb_acquire(tag)
            s_xr = wtile(w_in_d[cs[0]])
            sc.op("dve", lambda e: e.tensor_copy(xr[:, 0:3], xcarry[:, cs[0], :]), reads=[bxc], writes=xb_)
            proj(XB, s_xr, uT, uTB)
            yield
            sc.op("dve", lambda e: e.tensor_copy(xr[:, 3:515], ps[:, XB, :]), reads=[psB[XB]], writes=xb_)
            xb_release()
            yield
            for ci, c in enumerate(cs):
                cn = cs[ci + 1] if ci + 1 < len(cs) else None
                yield from xb_acquire(tag)
                s_zr = wtile(w_in_d[8 + c])
                proj(XB, s_zr, uT, uTB)
                sc.op("dve", lambda e, c=c: e.tensor_copy(xcarry[:, c, :], xr[:, 512:515]), reads=xb_, writes=[bxc])
                xc, xcB, xc_i = alloc()
                sc.op("dve", lambda e, xc=xc, c=c: e.tensor_scalar(
                    xc, xr[:, 3:515], convw[:, c, 3:4], vecs[:, c, 0:1], ALU.mult, ALU.add),
                    reads=xb_ + [cw, bv], writes=xcB)
                yield
                tz, tzB, tz_i = alloc()
                sc.op("act", lambda e, tz=tz: e.activation(tz, ps[:, XB, :], AF.Tanh, scale=0.5),
                      reads=[psB[XB]], writes=tzB)
                sc.op("dve", lambda e, xc=xc, c=c: e.scalar_tensor_tensor(
                    xc, xr[:, 2:514], convw[:, c, 2:3], xc, ALU.mult, ALU.add), reads=xb_ + [cw] + xcB, writes=xcB)
                yield
                sc.op("dve", lambda e, tz=tz, c=c: e.scalar_tensor_tensor(
                    yr[:, c, :], tz, 1.0, ps[:, XB, :], ALU.add, ALU.mult), reads=tzB + [psB[XB]], writes=[yrB[c]])
                xb_release()
                release(tz_i)
                sc.op("dve", lambda e, xc=xc, c=c: e.scalar_tensor_tensor(
                    xc, xr[:, 1:513], convw[:, c, 1:2], xc, ALU.mult, ALU.add), reads=xb_ + [cw] + xcB, writes=xcB)
                yield
                sc.op("dve", lambda e, xc=xc, c=c: e.scalar_tensor_tensor(
                    xc, xr[:, 0:512], convw[:, c, 0:1], xc, ALU.mult, ALU.add), reads=xb_ + [cw] + xcB, writes=xcB)
                xcb_, xcbB, xcb_i = alloc()
                xcb = xcb_.bitcast(BF16)[:, 0:T]
                sc.op("dve", lambda e, xcb=xcb, xc=xc: e.tensor_copy(xcb, xc), reads=xcB, writes=xcbB)
                yield
                yield from xb_acquire(tag)
                sc.op("pe", lambda e, c=c, xcb=xcb: e.matmul(ps[:, XB, :], wabd[:, c, :], xcb, start=True, stop=True),
                      reads=[buf("wabd")] + xcbB, writes=[psB[XB]])
                yield
                tr, trB, tr_i = alloc()
                sc.op("act", lambda e, tr=tr, c=c: e.activation(
                    tr, ps[:, XB, :], AF.Tanh, bias=small[:, K_HBA, c:c + 1], scale=0.5),
                    reads=[psB[XB], bsmall], writes=trB)
                xb_release()
                yield
                yield from xb_acquire(tag)
                sc.op("pe", lambda e, c=c, xcb=xcb: e.matmul(ps[:, XB, :], wxbd[:, c, :], xcb, start=True, stop=True),
                      reads=[buf("wxbd")] + xcbB, writes=[psB[XB]])
                release(xcb_i)
                a_, aB, a_i = alloc()
                sc.op("act", lambda e, a_=a_, tr=tr, c=c: e.activation(
                    a_, tr, AF.Exp, bias=small[:, K_CH, c:c + 1], scale=small[:, K_CH, c:c + 1]),
                    reads=trB + [bsmall], writes=aB)
                yield
                ti, tiB, ti_i = alloc()
                sc.op("act", lambda e, ti=ti, c=c: e.activation(
                    ti, ps[:, XB, :], AF.Tanh, bias=small[:, K_HBX, c:c + 1], scale=0.5),
                    reads=[psB[XB], bsmall], writes=tiB)
                xb_release()
                a2, a2B, a2_i = alloc()
                sc.op("act", lambda e, a2=a2, tr=tr, c=c: e.activation(
                    a2, tr, AF.Exp, bias=small[:, K_C, c:c + 1], scale=small[:, K_C, c:c + 1]),
                    reads=trB + [bsmall], writes=a2B)
                yield
                th, thB, th_i = alloc()
                sc.op("act", lambda e, th=th, tr=tr, c=c: e.activation(
                    th, tr, AF.Tanh, bias=small[:, K_CH, c:c + 1], scale=small[:, K_CH, c:c + 1]),
                    reads=trB + [bsmall], writes=thB)
                release(tr_i)
                sc.op("dve", lambda e, ti=ti, xc=xc: e.scalar_tensor_tensor(ti, ti, 1.0, xc, ALU.add, ALU.mult),
                      reads=tiB + xcB, writes=tiB)
                release(xc_i)
                yield
                sc.op("dve", lambda e, a2=a2, th=th: e.scalar_tensor_tensor(a2, a2, 1.0, th, ALU.add, ALU.mult),
                      reads=a2B + thB, writes=a2B)
                release(th_i)
                yield
                if cn is not None:
                    yield from xb_acquire(tag)
                sc.op("act", lambda e, a2=a2: e.activation(a2, a2, AF.Sqrt, scale=-0.25), reads=a2B, writes=a2B)
                if cn is not None:
                    s_xr = wtile(w_in_d[cn])
                    sc.op("dve", lambda e, cn=cn: e.tensor_copy(xr[:, 0:3], xcarry[:, cn, :]),
                          reads=[bxc], writes=xb_)
                    proj(XB, s_xr, uT, uTB)
                yield
                if cn is not None:
                    sc.op("dve", lambda e: e.tensor_copy(xr[:, 3:515], ps[:, XB, :]),
                          reads=[psB[XB]], writes=xb_)
                    xb_release()
                sc.op("dve", lambda e, ti=ti, a2=a2: e.tensor_tensor(ti, ti, a2, ALU.mult), reads=tiB + a2B, writes=tiB)
                release(a2_i)
                yield
                hh, hhB, hh_i = alloc()
                sc.op("dve", lambda e, hh=hh, a_=a_, ti=ti, c=c: e.tensor_tensor_scan(
                    hh, a_, ti, hstate[:, c:c + 1], ALU.mult, ALU.add), reads=aB + tiB + [bh], writes=hhB)
                release(a_i, ti_i)
                sc.op("dve", lambda e, hh=hh, c=c: e.tensor_copy(hstate[:, c:c + 1], hh[:, T - 1:T]),
                      reads=hhB, writes=[bh])
                yield
                sc.op("dve", lambda e, hh=hh, c=c: e.tensor_tensor(yr[:, c, :], hh, yr[:, c, :], ALU.mult),
                      reads=hhB + [yrB[c]], writes=[yrB[c]])
                release(hh_i)
                yield

        def epi_gen(h, ssb, ssbB, ss_i, o0, o0B, o0_i, o1, o1B, o1_i):
            sc.op("dve", lambda e: e.reciprocal(ssb[0:64, :], ssb[0:64, :]), reads=ssbB, writes=ssbB)
            yield
            yield from xb_acquire("epi")
            sc.op("pe", lambda e: e.matmul(ps[:, XB, :], self_[0:64, 0:P], ssb[0:64, :], start=True, stop=True),
                  reads=ssbB + [buf("self")], writes=[psB[XB]])
            yield
            sc.op("dve", lambda e: e.tensor_tensor(o0, o0, ps[:, XB, :], ALU.mult), reads=o0B + [psB[XB]], writes=o0B)
            xb_release()
            yield
            yield from xb_acquire("epi")
            sc.op("pe", lambda e: e.matmul(ps[:, XB, :], self_[0:64, P:2 * P], ssb[0:64, :], start=True, stop=True),
                  reads=ssbB + [buf("self")], writes=[psB[XB]])
            yield
            sc.op("dve", lambda e: e.tensor_tensor(o1, o1, ps[:, XB, :], ALU.mult), reads=o1B + [psB[XB]], writes=o1B)
            xb_release()
            sc.op("dve", lambda e: e.scalar_tensor_tensor(o0, o1, neg_lam, o0, ALU.mult, ALU.add),
                  reads=o0B + o1B + [bmisc], writes=o0B)
            yield
            sc.op("dve", lambda e: e.tensor_tensor(o1, o0, o0, ALU.mult), reads=o0B, writes=o1B)
            yield
            yield from xb_acquire("epi")
            sc.op("pe", lambda e: e.matmul(ps[:, XB, :], onesf[:], o1, start=True, stop=True),
                  reads=o1B + [buf("onesf")], writes=[psB[XB]])
            yield
            sc.op("dve", lambda e: e.tensor_copy(ssb, ps[:, XB, :]), reads=[psB[XB]], writes=ssbB)
            xb_release()
            ar_release(o1_i)
            sig["ln_ready"] = True
            want = sig["projk"]
            while bg["qk"] is not None and sig["projk"] == want:
                yield
            sig["ln_ready"] = False
            sc.op("act", lambda e: e.activation(ssb, ssb, AF.Ln, bias=NORM_EPS), reads=ssbB, writes=ssbB)
            sc.op("act", lambda e: e.activation(ssb, ssb, AF.Exp, scale=-0.5), reads=ssbB, writes=ssbB)
            yield
            sc.op("dve", lambda e: e.tensor_tensor(o0, o0, ssb, ALU.mult), reads=o0B + ssbB, writes=o0B)
            ar_release(ss_i)
            yield
            sc.op("dve", lambda e: e.scalar_tensor_tensor(ya[:, h, :], o0, subg, ya[:, h, :], ALU.mult, ALU.mult),
                  reads=o0B + [bmisc, yaB[h]], writes=[yaB[h]])
            ar_release(o0_i)
            yield

        xr2 = Vc[:, 24:26, :].rearrange("p a b -> p (a b)").bitcast(F32)[:, 0:516]
        xr2B = Buf("xr2")
        exB = [Buf("ex%d" % i) for i in range(6)]
        ex_free = list(range(6))

        def ex_alloc():
            assert ex_free, "extra arena exhausted"
            i = ex_free.pop()
            return Vc[:, 26 + i, :].bitcast(F32), [exB[i]], i

        def ex_release(*idx):
            for i in idx:
                assert i not in ex_free
                ex_free.append(i)

        bg = {"qk": None, "epi": None, "rnn": None, "rnn2": None}
        sig = {"ln_ready": False, "projk": 0}

        def pump(n=1):
            if nopump:
                return
            for _ in range(n):
                for k in ("qk", "epi", "rnn", "rnn2"):
                    g = bg[k]
                    if g is not None:
                        try:
                            next(g)
                        except StopIteration:
                            bg[k] = None

        def drain(k):
            while bg[k] is not None:
                for kk in ("qk", "epi", "rnn", "rnn2"):
                    g = bg[kk]
                    if g is not None:
                        try:
                            next(g)
                        except StopIteration:
                            bg[kk] = None

        for j in range(NCH):
            tok0 = j * T
            state["pslim"] = 8
            pg, pgB, pgi = ar_alloc2()
            pgf = pg.rearrange("p a b -> p (a b)")
            sc.dma("sp", lambda e, pgf=pgf: e.dma_start(out=pgf, in_=preg_d[:, :]), arC[pgi], writes=pgB)
            s1 = {}

            def s1_front(tt):
                r0 = tok0 + tt * P
                xt, xtB, xi = ar_alloc2()
                xf = xt.rearrange("p a b -> p (a b)")
                sc.dma("sp", lambda e: e.dma_start(out=xf, in_=x_d[r0:r0 + P, :]), arC[xi], writes=xtB)
                ub, ubB, ub_i = ar_alloc()
                ubb = ub.bitcast(BF16)
                ss, ssB = st_get()
                sc.op("act", lambda e: e.activation(ubb, xf, AF.Square, accum_out=ss), reads=xtB, writes=ubB + ssB)
                sd, sdB = st_get()
                sc.op("act", lambda e: e.activation(sd, ss, AF.Sqrt, bias=NORM_EPS, scale=1.0 / D),
                      reads=ssB, writes=sdB)
                rs, rsB = st_get()
                sc.op("dve", lambda e: e.reciprocal(rs, sd), reads=sdB, writes=rsB)
                sc.op("dve", lambda e, pgf=pgf: e.scalar_tensor_tensor(ubb, xf, rs, pgf, ALU.mult, ALU.mult),
                      reads=xtB + rsB + pgB + ubB, writes=ubB)
                s1[tt] = (xi, ubb, ubB, ub_i)

            def s1_back(tt):
                xi, ubb, ubB, ub_i = s1.pop(tt)
                ar_release(xi, xi + 1)
                bank = ps_get()
                pst = ps[:, bank, :].bitcast(BF16)
                for dt in range(8):
                    sc.op("pe", lambda e, dt=dt: e.transpose(
                        pst[:, dt * P:(dt + 1) * P], ubb[:, dt * P:(dt + 1) * P], ident[:]),
                        reads=ubB + [buf("ident")], writes=[psB[bank]], flag=(dt == 7))
                ar_release(ub_i)
                sc.op("dve", lambda e: e.tensor_copy(
                    uT[:, :, tt * P:(tt + 1) * P], pst.rearrange("p (a b) -> p a b", a=8)),
                    reads=[psB[bank]], writes=[uTB[tt]])

            s1_front(0)
            s1_front(1)
            s1_back(0)
            s1_front(2)
            s1_back(1)
            s1_front(3)
            s1_back(2)
            s1_back(3)
            ar_release(pgi, pgi + 1)

            n_steps = 16 + 8 * (4 * j + 4)
            if j <= 5:
                bg["rnn"] = rnn_gen([0, 2, 4, 6], ar_alloc, ar_release, xrs[:, 0, :], [xrB[0]], "rnn")
                bg["rnn2"] = rnn_gen([1, 3, 5, 7], ex_alloc, ex_release, xr2, [xr2B], "rnn2")
                npump = max(1, -(-(4 * 20) // n_steps))
            else:
                bg["rnn"] = rnn_gen(list(range(8)), ar_alloc, ar_release, xrs[:, 0, :], [xrB[0]], "rnn")
                npump = max(1, -(-(8 * 20) // n_steps))
            state["pslim"] = 6
            state["ps"] = 0

            for h in range(8):
                s_za = wtile(w_in_d[40 + h])
                b_za = ps_get()
                proj(b_za, s_za, uT, uTB)
                tza, tzaB, tza_i = ar_alloc()
                sc.op("act", lambda e, tza=tza, b_za=b_za: e.activation(tza, ps[:, b_za, :], AF.Tanh, scale=0.5),
                      reads=[psB[b_za]], writes=tzaB)
                sc.op("dve", lambda e, tza=tza, b_za=b_za, h=h: e.scalar_tensor_tensor(
                    ya[:, h, :], tza, 1.0, ps[:, b_za, :], ALU.add, ALU.mult),
                    reads=tzaB + [psB[b_za]], writes=[yaB[h]])
                ar_release(tza_i)
                pump(npump)
            for h in range(8):
                s_v = wtile(w_in_d[32 + h])
                b_v = ps_get()
                for tt in range(4):
                    for dt in range(8):
                        sc.op("pe", lambda e, b_v=b_v, tt=tt, dt=dt, s_v=s_v: e.matmul(
                            ps[:, b_v, tt * P:(tt + 1) * P], uT[:, dt, tt * P:(tt + 1) * P], wring[:, s_v, dt, :],
                            start=(dt == 0), stop=(dt == 7)),
                            reads=[wrB[s_v], uTB[tt]], writes=[psB[b_v]], flag=(dt == 7))
                sc.op("dve", lambda e, b_v=b_v, h=h, j=j: e.tensor_copy(
                    Vc[:, 4 * j:4 * j + 4, h * P:(h + 1) * P], ps[:, b_v, :].rearrange("p (a b) -> p a b", a=4)),
                    reads=[psB[b_v]], writes=[VB[h][j]] + (([xr2B] + exB) if j >= 6 else []))
                pump(npump)
            nkt = 4 * j + 4

            def qk_inline(h):
                s_q = wtile(w_in_d[16 + h])
                s_k = wtile(w_in_d[24 + h])
                proj(0, s_q, uT, uTB)
                proj(2, s_k, uT, uTB)
                qq = h % 2
                sc.op("dve", lambda e: e.tensor_copy(qT[:, qq, :], ps[:, 0, :]), reads=[psB[0]], writes=[qB[qq]])
                sc.op("dve", lambda e, tok0=tok0: e.tensor_copy(Kc[:, h, tok0:tok0 + T], ps[:, 2, :]),
                      reads=[psB[2]], writes=[KB[h][j]])

            def qk_gen(h):
                qq = h % 2
                yield from xb_acquire("qk")
                s_q = wtile(w_in_d[16 + h])
                proj(XB, s_q, uT, uTB)
                yield
                sc.op("dve", lambda e: e.tensor_copy(qT[:, qq, :], ps[:, XB, :]), reads=[psB[XB]], writes=[qB[qq]])
                xb_release()
                yield
                while bg["epi"] is not None and not sig["ln_ready"]:
                    yield
                yield from xb_acquire("qk")
                s_k = wtile(w_in_d[24 + h])
                proj(XB, s_k, uT, uTB)
                sig["projk"] += 1
                yield
                sc.op("dve", lambda e, tok0=tok0: e.tensor_copy(Kc[:, h, tok0:tok0 + T], ps[:, XB, :]),
                      reads=[psB[XB]], writes=[KB[h][j]])
                xb_release()
                yield

            def qk(h, kt, gi):
                i = kt - 4 * j
                lo = P * i if i > 0 else 0
                pr = gi % 2
                qq = h % 2
                for c2 in range(2):
                    bank = 2 * pr + c2
                    sc.op("pe", lambda e, bank=bank, c2=c2: e.matmul(
                        ps[:, bank, lo:T], Kc[64 * c2:64 * c2 + 64, h, kt * P:(kt + 1) * P],
                        qT[64 * c2:64 * c2 + 64, qq, lo:T], start=True, stop=True),
                        reads=[KB[h][kt // 4], qB[qq]], writes=[psB[bank]])

            steps = [(h, kt) for h in range(8) for kt in range(nkt)]
            qk_inline(0)
            qk(0, 0, 0)
            for gi, (h, kt) in enumerate(steps):
                if kt == 0 and h + 1 < 8:
                    bg["qk"] = qk_gen(h + 1)
                if gi + 1 < len(steps):
                    h2, kt2 = steps[gi + 1]
                    if kt2 == 0:
                        drain("qk")
                    qk(h2, kt2, gi + 1)
                i = kt - 4 * j
                lo = P * i if i > 0 else 0
                pr = gi % 2
                ei = state["e"]
                state["e"] = (ei + 1) % NE
                sc.op("act", lambda e, ei=ei, pr=pr, lo=lo: e.activation(
                    Et[:, ei, :, lo:T], ps[:, 2 * pr:2 * pr + 2, lo:T], AF.Exp, scale=0.125),
                    reads=[psB[2 * pr], psB[2 * pr + 1]], writes=[EB[ei]])
                if i >= 0:
                    for c2 in range(2):
                        sc.op("dve", lambda e, ei=ei, c2=c2, lo=lo: e.tensor_tensor(
                            Et[:, ei, c2, lo:lo + P], Et[:, ei, c2, lo:lo + P], tri[:], ALU.mult),
                            reads=[EB[ei], buf("tri")], writes=[EB[ei]])
                first, last = (kt == 0), (kt == nkt - 1)
                for c2 in range(2):
                    sc.op("pe", lambda e, ei=ei, c2=c2, lo=lo, first=first, last=last: e.matmul(
                        ps[32 * c2:32 * c2 + 32, 6, lo:T], selb[:, 32 * c2:32 * c2 + 32], Et[:, ei, c2, lo:T],
                        start=first, stop=last),
                        reads=[EB[ei], buf("selb")], writes=[psB[6]], flag=False)
                for c2 in range(2):
                    sc.op("pe", lambda e, ei=ei, c2=c2, kt=kt, lo=lo, first=first, last=last, h=h: e.matmul(
                        ps[:, 4 + c2, lo:T], Vc[:, kt, h * P:(h + 1) * P], Et[:, ei, c2, lo:T],
                        start=first, stop=last),
                        reads=[EB[ei], VB[h][kt // 4]], writes=[psB[4 + c2]], flag=(c2 == 1))
                pump(npump)
                if last:
                    drain("epi")
                    ssb, ssbB, ss_i = ar_alloc()
                    o0, o0B, o0_i = ar_alloc()
                    o1, o1B, o1_i = ar_alloc()
                    sc.op("dve", lambda e, ssb=ssb: e.tensor_copy(ssb[0:64, :], ps[0:64, 6, :]),
                          reads=[psB[6]], writes=ssbB)
                    sc.op("act", lambda e, o0=o0: e.activation(o0, ps[:, 4, :], AF.Copy), reads=[psB[4]], writes=o0B)
                    sc.op("dve", lambda e, o1=o1: e.tensor_copy(o1, ps[:, 5, :]), reads=[psB[5]], writes=o1B)
                    bg["epi"] = epi_gen(h, ssb, ssbB, ss_i, o0, o0B, o0_i, o1, o1B, o1_i)
            drain("rnn")
            drain("rnn2")
            state["pslim"] = 7
            state["ps"] = 0

            s5x = {}

            def s5_load(tt):
                r0 = tok0 + tt * P
                xt, xtB, xi = ar_alloc2()
                xf = xt.rearrange("p a b -> p (a b)")
                sc.dma("sp", lambda e: e.dma_start(out=xf, in_=x_d[r0:r0 + P, :]), arC[xi], writes=xtB)
                s5x[tt] = (xt, xf, xtB, xi)

            for dt in range(8):
                s_gr = wtile(w_in_d[48 + dt])
                s_ga = wtile(w_in_d[56 + dt])
                s_br = wtile(w_brr_d[dt])
                s_ba = wtile(w_bra_d[dt])
                b_gr = ps_get()
                proj(b_gr, s_gr, uT, uTB)
                pump(2)
                b_ga = ps_get()
                proj(b_ga, s_ga, uT, uTB)
                pump(2)
                b_pr = ps_get()
                proj(b_pr, s_br, yr, yrB)
                if dt == 0:
                    drain("epi")
                    pg, pgB, pgi = ar_alloc2()
                    sc.dma("sp", lambda e, pg=pg: e.dma_start(out=pg.rearrange("p a b -> p (a b)"), in_=postg_d[:, :]),
                           arC[pgi], writes=pgB)
                    s5_load(0)
                    s5_load(1)
                b_pa = ps_get()
                proj(b_pa, s_ba, ya, yaB)
                tg, tgB, tg_i = ar_alloc()
                tg2, tg2B, tg2_i = ar_alloc()
                sc.op("act", lambda e, tg=tg, b_gr=b_gr: e.activation(tg, ps[:, b_gr, :], AF.Tanh, scale=0.5),
                      reads=[psB[b_gr]], writes=tgB)
                sc.op("act", lambda e, tg2=tg2, b_ga=b_ga: e.activation(tg2, ps[:, b_ga, :], AF.Tanh, scale=0.5),
                      reads=[psB[b_ga]], writes=tg2B)
                sc.op("dve", lambda e, tg=tg, b_pr=b_pr: e.scalar_tensor_tensor(
                    tg, tg, 1.0, ps[:, b_pr, :], ALU.add, ALU.mult), reads=tgB + [psB[b_pr]], writes=tgB)
                sc.op("dve", lambda e, tg2=tg2, b_pa=b_pa: e.scalar_tensor_tensor(
                    tg2, tg2, 1.0, ps[:, b_pa, :], ALU.add, ALU.mult), reads=tg2B + [psB[b_pa]], writes=tg2B)
                sc.op("dve", lambda e, tg=tg, tg2=tg2, dt=dt: e.tensor_tensor(mT[:, dt, :], tg, tg2, ALU.add),
                      reads=tgB + tg2B, writes=[mTB[dt]])
                ar_release(tg_i, tg2_i)
            state["pslim"] = 8

            for ft in range(8):
                s_o = wtile(w_out_d[ft])
                for tt in range(4):
                    bank = 2 * tt + ft // 4
                    c0 = (ft % 4) * P
                    for dt in range(8):
                        sc.op("pe", lambda e, bank=bank, c0=c0, dt=dt, tt=tt, s_o=s_o: e.matmul(
                            ps[:, bank, c0:c0 + P], mT[:, dt, tt * P:(tt + 1) * P], wring[:, s_o, dt, :],
                            start=(dt == 0), stop=(dt == 7)),
                            reads=[mTB[dt], wrB[s_o]], writes=[psB[bank]], flag=(dt == 7))
            junk, junkB, junk_i = ar_alloc()
            junkb = junk.bitcast(BF16).rearrange("p (a b) -> p a b", a=2)
            s5r = {}

            def s5_front(tt):
                yps = ps[:, 2 * tt:2 * tt + 2, :]
                ypsB = [psB[2 * tt], psB[2 * tt + 1]]
                ss, ssB = st_get()
                sc.op("act", lambda e, junkb=junkb: e.activation(junkb, yps, AF.Square, accum_out=ss), reads=ypsB, writes=junkB + ssB)
                sd, sdB = st_get()
                sc.op("act", lambda e: e.activation(sd, ss, AF.Sqrt, bias=16.0 * NORM_EPS, scale=1.0 / D),
                      reads=ssB, writes=sdB)
                rs, rsB = st_get()
                sc.op("dve", lambda e: e.reciprocal(rs, sd), reads=sdB, writes=rsB)
                s5r[tt] = (rs, rsB)

            def s5_back(tt):
                r0 = tok0 + tt * P
                yps = ps[:, 2 * tt:2 * tt + 2, :]
                ypsB = [psB[2 * tt], psB[2 * tt + 1]]
                rs, rsB = s5r.pop(tt)
                xt, xf, xtB, xi = s5x.pop(tt)
                sc.op("dve", lambda e, pg=pg: e.tensor_tensor(yps, yps, pg, ALU.mult), reads=ypsB + pgB, writes=ypsB)
                sc.op("dve", lambda e: e.scalar_tensor_tensor(xt, yps, rs, xt, ALU.mult, ALU.add),
                      reads=ypsB + rsB + xtB, writes=xtB)
                sc.dma("sp", lambda e: e.dma_start(out=out_d[r0:r0 + P, :], in_=xf), arC[xi], reads=xtB)
                ar_release(xi, xi + 1)

            s5_front(0)
            s5_load(2)
            s5_front(1)
            s5_back(0)
            s5_load(3)
            s5_front(2)
            s5_back(1)
            s5_front(3)
            s5_back(2)
            s5_back(3)
            ar_release(junk_i)
            ar_release(pgi, pgi + 1)

        if debug:
            c_dbg = sc.new_counter("cdbg")
            def dump(name, t_ap, shape, dt, bufs):
                d = nc.dram_tensor(name, list(shape), dt, kind="ExternalOutput").ap()
                sc.dma("sp", lambda e: e.dma_start(out=d, in_=t_ap), c_dbg, reads=bufs)
            dump("dbg_uT", uT[:], [P, 8, T], BF16, uTB)
            dump("dbg_yr", yr[:], [P, 8, T], BF16, yrB)
            dump("dbg_ya", ya[:], [P, 8, T], BF16, yaB)
            dump("dbg_mT", mT[:], [P, 8, T], BF16, mTB)
            dump("dbg_Kc", Kc[:], [P, 8, S], BF16, [b for l in KB for b in l])
            dump("dbg_Vc", Vc[:], [P, S // P, D], BF16, [b for l in VB for b in l])
            dump("dbg_small", small[:], [P, 8, 8], F32, [bsmall])
            dump("dbg_misc", misc[:], [P, 16], F32, [bmisc])
            dump("dbg_hstate", hstate[:], [P, 8], F32, [bh])
            dump("dbg_stat", stat[:], [P, 64], F32, statB)
            sc.wait_all("sp", [(c_dbg, c_dbg.val)])
        sc.wait_all("sp", [(c, c.val) for c in arC])

        for c in sc.counters:
            c.sem = es.enter_context(nc.semaphore(c.name))
        block = es.enter_context(nc.Block())
        sc.emit(block)
    return nc


def _tile_w(w, nft):
    return np.ascontiguousarray(w.reshape(8, P, nft, 128).transpose(2, 1, 0, 3))


def _selb():
    return np.ones((P, 64), np.float32).astype(ml_dtypes.bfloat16)


def _self():
    m = np.zeros((64, 2 * P), np.float32)
    m[0, 0:P] = 1.0
    m[32, P:2 * P] = 1.0
    return m


def _blockdiag(w):
    out = np.zeros((P, 8, P), np.float32)
    for g in range(16):
        c, o = g // 2, (g % 2) * 64
        out[o:o + 64, c, o:o + 64] = w[g]
    return out


def kernel(x, pre_g, post_g, w_in, conv_w, conv_b, lru_wa, lru_ba, lru_wx, lru_bx,
           lru_a, attn_lq1, attn_lk1, attn_lq2, attn_lk2, subln_g, w_br_rnn,
           w_br_attn, w_out):
    f32 = np.float32
    x = np.asarray(x, f32)
    shared = {
        "w_in_t": _tile_w(np.asarray(w_in[0], f32), 64),
        "w_br_rnn_t": _tile_w(np.asarray(w_br_rnn[0], f32), 8),
        "w_br_attn_t": _tile_w(np.asarray(w_br_attn[0], f32), 8),
        "w_out_t": _tile_w(np.asarray(w_out[0], f32), 8),
        "pre_g_b": np.ascontiguousarray(np.broadcast_to(np.asarray(pre_g[0], f32), (P, D))),
        "post_g_b": np.ascontiguousarray(np.broadcast_to(np.asarray(post_g[0], f32), (P, D))),
        "conv_wt": np.ascontiguousarray(np.asarray(conv_w[0], f32).reshape(4, 8, P).transpose(2, 1, 0)),
        "vecs": np.ascontiguousarray(np.stack([
            np.asarray(conv_b[0], f32).reshape(8, P),
            np.asarray(lru_ba[0], f32).reshape(8, P),
            np.asarray(lru_bx[0], f32).reshape(8, P),
            np.asarray(lru_a[0], f32).reshape(8, P)], axis=-1).transpose(1, 0, 2)),
        "wa_bd": _blockdiag(np.asarray(lru_wa[0], f32)),
        "wx_bd": _blockdiag(np.asarray(lru_wx[0], f32)),
        "lam_qk": np.ascontiguousarray(np.broadcast_to(np.stack([
            np.asarray(attn_lq1[0], f32), np.asarray(attn_lk1[0], f32),
            np.asarray(attn_lq2[0], f32), np.asarray(attn_lk2[0], f32)], axis=0), (P, 4, 64))),
        "subln_g": np.ascontiguousarray(np.asarray(subln_g[0], f32).reshape(P, 1)),
        "ident": np.eye(P, dtype=f32).astype(ml_dtypes.bfloat16),
        "tri": np.triu(np.ones((P, P), f32)).astype(ml_dtypes.bfloat16),
        "ones_bf": np.ones((P, P), f32).astype(ml_dtypes.bfloat16),
        "ones_f": np.full((P, P), 1.0 / P, f32),
        "selb": _selb(),
        "self": _self(),
    }
    nc = build_program()
    in_maps = []
    for b in range(8):
        m = dict(shared)
        m["x"] = np.ascontiguousarray(x[b])
        in_maps.append(m)
    res = run_bass_kernel_spmd(nc, in_maps, core_ids=list(range(8)))
    return np.stack([np.asarray(r["out"], f32) for r in res.results], axis=0)
```
